# Optimizing a Trainium2 kernel written in Bass

```python
import jax, jax.numpy as jnp
from jax import lax
import numpy as np

D_MODEL = 1024
BATCH = 4
SEQ = 8192
DEPTH = 2

GRID_W = 64
CTX_LEN = 256
N_EVEN = (DEPTH + 1) // 2
N_ODD = DEPTH // 2

HEAD_DIM = 64
ATT_HEADS = D_MODEL // (2 * HEAD_DIM)
KV_HEADS = ATT_HEADS // 4
Q_DIM = ATT_HEADS * HEAD_DIM
KV_DIM = KV_HEADS * HEAD_DIM
ATT_COLS = Q_DIM + 2 * KV_DIM
WINDOW = 128
BLOCK = 128
SPAN = BLOCK + 2 * WINDOW
AXIS_DIM = HEAD_DIM // 2
ROPE_BASE = 10000.0

RWKV_N = 64
RWKV_HEADS = D_MODEL // (2 * RWKV_N)
RWKV_DIM = RWKV_HEADS * RWKV_N
DECAY_LORA = 64
ICLR_LORA = 64
GATE_LORA = 128
RWKV_COLS = 3 * RWKV_DIM + DECAY_LORA + ICLR_LORA + GATE_LORA
RWKV_SPLITS = [RWKV_DIM, 2 * RWKV_DIM, 3 * RWKV_DIM,
               3 * RWKV_DIM + DECAY_LORA, 3 * RWKV_DIM + DECAY_LORA + ICLR_LORA]
IN_COLS = ATT_COLS + RWKV_COLS

FOURIER_GROUPS = 4

D_FF = 4 * D_MODEL
N_MOD = 6
NORM_EPS = 1e-6
GN_EPS = 64e-5
NEG_INF = -1e30

kernel_name = "hybrid_swa_rwkv7_fnet_dit"


def rmsnorm(z, g):
    zf = z.astype(jnp.float32)
    zf = zf * lax.rsqrt(jnp.mean(zf * zf, axis=-1, keepdims=True) + NORM_EPS)
    return zf.astype(z.dtype) * g


def modulate(h, shift, scale):
    return h * (1 + scale) + shift


def axial_angles(T):
    rows = T // GRID_W
    row = jnp.broadcast_to(jnp.arange(rows, dtype=jnp.float32)[:, None], (rows, GRID_W)).reshape(T)
    col = jnp.broadcast_to(jnp.arange(GRID_W, dtype=jnp.float32)[None, :], (rows, GRID_W)).reshape(T)
    inv = ROPE_BASE ** (-jnp.arange(0, AXIS_DIM, 2, dtype=jnp.float32) / AXIS_DIM)
    return row[:, None] * inv, col[:, None] * inv


def rope_half(x, ang):
    cos = jnp.cos(ang)[None, :, None, :].astype(x.dtype)
    sin = jnp.sin(ang)[None, :, None, :].astype(x.dtype)
    x1, x2 = x[..., :AXIS_DIM // 2], x[..., AXIS_DIM // 2:]
    return jnp.concatenate([x1 * cos - x2 * sin, x1 * sin + x2 * cos], axis=-1)


def axial_rope(x, ang_r, ang_c):
    return jnp.concatenate([rope_half(x[..., :AXIS_DIM], ang_r),
                            rope_half(x[..., AXIS_DIM:], ang_c)], axis=-1)


def window_attention(q, k, v, kc, vc, sink):
    B, T = q.shape[0], q.shape[1]
    C = kc.shape[1]
    G = ATT_HEADS // KV_HEADS
    nb = T // BLOCK
    scale = HEAD_DIM ** -0.5
    qb = jnp.moveaxis(q.reshape(B, nb, BLOCK, KV_HEADS, G, HEAD_DIM), 1, 0)
    pad = ((0, 0), (WINDOW, WINDOW), (0, 0), (0, 0))
    kp = jnp.pad(k, pad)
    vp = jnp.pad(v, pad)
    sink_logit = jnp.broadcast_to(sink.astype(jnp.float32).reshape(1, KV_HEADS, G, 1, 1),
                                  (B, KV_HEADS, G, BLOCK, 1))
    offs_q = jnp.arange(BLOCK)
    offs_k = jnp.arange(SPAN) - WINDOW

    def one_block(args):
        i, qi = args
        start = i * BLOCK
        ki = lax.dynamic_slice_in_dim(kp, start, SPAN, axis=1)
        vi = lax.dynamic_slice_in_dim(vp, start, SPAN, axis=1)
        qpos = start + offs_q
        kpos = start + offs_k
        valid = ((jnp.abs(kpos[None, :] - qpos[:, None]) <= WINDOW)
                 & (kpos >= 0)[None, :] & (kpos < T)[None, :])
        s_loc = jnp.einsum('bqhgd,bkhd->bhgqk', qi, ki).astype(jnp.float32) * scale
        s_loc = jnp.where(valid, s_loc, NEG_INF)
        s_ctx = jnp.einsum('bqhgd,bkhd->bhgqk', qi, kc).astype(jnp.float32) * scale
        p = jax.nn.softmax(jnp.concatenate([s_loc, s_ctx, sink_logit], axis=-1), axis=-1).astype(vi.dtype)
        return (jnp.einsum('bhgqk,bkhd->bqhgd', p[..., :SPAN], vi)
                + jnp.einsum('bhgqk,bkhd->bqhgd', p[..., SPAN:SPAN + C], vc))

    out = lax.map(one_block, (jnp.arange(nb), qb))
    return jnp.moveaxis(out, 0, 1).reshape(B, T, Q_DIM)


def context_attention(qc, kc, vc, sink):
    B, C = qc.shape[0], qc.shape[1]
    G = ATT_HEADS // KV_HEADS
    qg = qc.reshape(B, C, KV_HEADS, G, HEAD_DIM)
    s = jnp.einsum('bqhgd,bkhd->bhgqk', qg, kc).astype(jnp.float32) * HEAD_DIM ** -0.5
    sink_logit = jnp.broadcast_to(sink.astype(jnp.float32).reshape(1, KV_HEADS, G, 1, 1),
                                  (B, KV_HEADS, G, C, 1))
    p = jax.nn.softmax(jnp.concatenate([s, sink_logit], axis=-1), axis=-1)[..., :C].astype(vc.dtype)
    return jnp.einsum('bhgqk,bkhd->bqhgd', p, vc).reshape(B, C, Q_DIM)


def centred_shift(z, mu_prev, mu_next):
    prev = jnp.pad(z, ((0, 0), (1, 0), (0, 0)))[:, :-1]
    nxt = jnp.pad(z, ((0, 0), (0, 1), (0, 0)))[:, 1:]
    return z + mu_prev * (prev - z) + mu_next * (nxt - z)


def rwkv_prep(zb, w0, w2, a0, a2, k_k, k_a):
    zb = zb.astype(jnp.float32)
    B, L = zb.shape[0], zb.shape[1]
    r, k, v, wl, al, gl = jnp.split(zb, RWKV_SPLITS, axis=-1)
    heads = lambda t: t.reshape(B, L, RWKV_HEADS, RWKV_N)
    kk = heads(k * k_k)
    kk = kk * lax.rsqrt(jnp.sum(kk * kk, axis=-1, keepdims=True) + 1e-12)
    dirs = []
    for d in range(2):
        w_log = -jax.nn.softplus(-(w0[d] + jnp.tanh(wl) @ w2[d])) - 0.5
        decay = jnp.exp(-jnp.exp(w_log))
        a = jax.nn.sigmoid(a0[d] + al @ a2[d])
        kd = k * (1 + (a - 1) * k_a)
        dirs.append((heads(decay), heads(kd), heads(a)))
    return heads(r), heads(k), heads(v), kk, dirs, gl


def rwkv_scan(state0, r, decay, k, v, kk, a, reverse):
    def step(S, inp):
        r_t, w_t, k_t, v_t, kk_t, a_t = inp
        s_kk = jnp.einsum('bhvk,bhk->bhv', S, kk_t)
        S = (S * w_t[:, :, None, :]
             - s_kk[..., None] * (kk_t * a_t)[:, :, None, :]
             + v_t[..., None] * k_t[:, :, None, :])
        return S, jnp.einsum('bhvk,bhk->bhv', S, r_t)
    xs = tuple(jnp.moveaxis(t, 1, 0) for t in (r, decay, k, v, kk, a))
    state, ys = lax.scan(step, state0, xs, reverse=reverse)
    return state, jnp.moveaxis(ys, 0, 1)


def rwkv_readout(y, r, k, v, gl, g2, r_k, lnx_g, lnx_b):
    B, L = y.shape[0], y.shape[1]
    mean = jnp.mean(y, axis=-1, keepdims=True)
    var = jnp.mean(jnp.square(y - mean), axis=-1, keepdims=True)
    yn = ((y - mean) * lax.rsqrt(var + GN_EPS)).reshape(B, L, RWKV_DIM) * lnx_g + lnx_b
    bonus = (jnp.sum(r * k * r_k, axis=-1, keepdims=True) * v).reshape(B, L, RWKV_DIM)
    g = jax.nn.sigmoid(gl) @ g2
    return (yn + bonus) * g


def rwkv_mixer(zb, zbc, w0, w2, a0, a2, g2, k_k, k_a, r_k, lnx_g, lnx_b, need_ctx):
    rc, kc, vc, kkc, dirs_c, glc = rwkv_prep(zbc, w0, w2, a0, a2, k_k, k_a)
    r, k, v, kk, dirs, gl = rwkv_prep(zb, w0, w2, a0, a2, k_k, k_a)
    B = zb.shape[0]
    s0 = jnp.zeros((B, RWKV_HEADS, RWKV_N, RWKV_N), jnp.float32)
    y = 0.0
    yc = 0.0
    for d, rev in ((0, False), (1, True)):
        dec_c, kd_c, a_c = dirs_c[d]
        s_ctx, yc_d = rwkv_scan(s0, rc, dec_c, kd_c, vc, kkc, a_c, rev)
        dec, kd, a = dirs[d]
        _, y_d = rwkv_scan(s_ctx, r, dec, kd, v, kk, a, rev)
        y = y + y_d
        yc = yc + yc_d
    out = rwkv_readout(y, r, k, v, gl, g2, r_k, lnx_g, lnx_b)
    out_c = rwkv_readout(yc, rc, kc, vc, glc, g2, r_k, lnx_g, lnx_b) if need_ctx else None
    return out, out_c


def hybrid_ab_mixer(h, hc, w_in, w_out, sink, mu_prev, mu_next, w0, w2, a0, a2, g2,
                    k_k, k_a, r_k, lnx_g, lnx_b, ang_r, ang_c, need_ctx):
    B, T = h.shape[0], h.shape[1]
    C = hc.shape[1]
    z = h @ w_in
    zc = hc @ w_in
    q = axial_rope(z[..., :Q_DIM].reshape(B, T, ATT_HEADS, HEAD_DIM), ang_r, ang_c)
    k = axial_rope(z[..., Q_DIM:Q_DIM + KV_DIM].reshape(B, T, KV_HEADS, HEAD_DIM), ang_r, ang_c)
    v = z[..., Q_DIM + KV_DIM:ATT_COLS].reshape(B, T, KV_HEADS, HEAD_DIM)
    qc = zc[..., :Q_DIM].reshape(B, C, ATT_HEADS, HEAD_DIM)
    kc = zc[..., Q_DIM:Q_DIM + KV_DIM].reshape(B, C, KV_HEADS, HEAD_DIM)
    vc = zc[..., Q_DIM + KV_DIM:ATT_COLS].reshape(B, C, KV_HEADS, HEAD_DIM)
    att = window_attention(q, k, v, kc, vc, sink)
    rw, rwc = rwkv_mixer(centred_shift(z[..., ATT_COLS:], mu_prev, mu_next),
                         centred_shift(zc[..., ATT_COLS:], mu_prev, mu_next),
                         w0, w2, a0, a2, g2, k_k, k_a, r_k, lnx_g, lnx_b, need_ctx)
    out = jnp.concatenate([att, rw.astype(att.dtype)], axis=-1) @ w_out
    if need_ctx:
        att_c = context_attention(qc, kc, vc, sink)
        out_c = jnp.concatenate([att_c, rwc.astype(att_c.dtype)], axis=-1) @ w_out
    else:
        out_c = None
    return out, out_c


def fourier_mixer(h, w_out):
    B, L, D = h.shape
    hg = h.astype(jnp.float32).reshape(B, L, FOURIER_GROUPS, D // FOURIER_GROUPS)
    f = jnp.fft.fft2(hg, axes=(1, 3), norm="ortho").real
    return f.reshape(B, L, D).astype(h.dtype) @ w_out


def channel_mlp(h, w1, w2):
    return jnp.square(jax.nn.relu(h @ w1)) @ w2


def setup_inputs(seed: int = 0) -> dict:
    key = jax.random.key(seed)
    ks = iter(jax.random.split(key, 40))
    nrm = lambda shape, s: jax.random.normal(next(ks), shape, jnp.float32) * s
    uni = lambda shape: jax.random.uniform(next(ks), shape, jnp.float32, 0.0, 0.6)
    D = D_MODEL
    return {
        "x": nrm((BATCH, SEQ, D), 1.0),
        "c": nrm((BATCH, D), 1.0),
        "ctx": nrm((BATCH, CTX_LEN, D), 1.0),
        "c_ctx": nrm((D,), 1.0),
        "ada_w": nrm((DEPTH, D, N_MOD * D), 0.5 * D ** -0.5),
        "ada_b": nrm((DEPTH, N_MOD * D), 0.01),
        "norm1_g": 1.0 + nrm((DEPTH, D), 0.02),
        "norm2_g": 1.0 + nrm((DEPTH, D), 0.02),
        "mix_w_in": nrm((N_EVEN, D, IN_COLS), D ** -0.5),
        "mix_w_out": nrm((N_EVEN, Q_DIM + RWKV_DIM, D), (Q_DIM + RWKV_DIM) ** -0.5),
        "attn_sink": nrm((N_EVEN, ATT_HEADS), 0.5),
        "shift_mu_prev": uni((N_EVEN, RWKV_COLS)),
        "shift_mu_next": uni((N_EVEN, RWKV_COLS)),
        "decay_w0": nrm((N_EVEN, 2, RWKV_DIM), 0.5),
        "decay_w2": nrm((N_EVEN, 2, DECAY_LORA, RWKV_DIM), 0.1),
        "iclr_a0": nrm((N_EVEN, 2, RWKV_DIM), 0.5),
        "iclr_a2": nrm((N_EVEN, 2, ICLR_LORA, RWKV_DIM), 0.1),
        "gate_g2": nrm((N_EVEN, GATE_LORA, RWKV_DIM), GATE_LORA ** -0.5),
        "key_kk": 0.85 + nrm((N_EVEN, RWKV_DIM), 0.05),
        "key_ka": 1.0 + nrm((N_EVEN, RWKV_DIM), 0.05),
        "bonus_rk": nrm((N_EVEN, RWKV_HEADS, RWKV_N), 0.1),
        "lnx_g": 1.0 + nrm((N_EVEN, RWKV_DIM), 0.02),
        "lnx_b": nrm((N_EVEN, RWKV_DIM), 0.01),
        "fourier_w_out": nrm((N_ODD, D, D), D ** -0.5),
        "mlp_w1": nrm((DEPTH, D, D_FF), D ** -0.5),
        "mlp_w2": nrm((DEPTH, D_FF, D), D_FF ** -0.5),
        "final_g": 1.0 + nrm((D,), 0.02),
    }


def reference(x, c, ctx, c_ctx, ada_w, ada_b, norm1_g, norm2_g, mix_w_in, mix_w_out,
              attn_sink, shift_mu_prev, shift_mu_next, decay_w0, decay_w2, iclr_a0, iclr_a2,
              gate_g2, key_kk, key_ka, bonus_rk, lnx_g, lnx_b, fourier_w_out, mlp_w1, mlp_w2,
              final_g):
    T = x.shape[1]
    ang_r, ang_c = axial_angles(T)
    s_lat = jax.nn.silu(c)
    s_ctx = jax.nn.silu(c_ctx)
    for i in range(DEPTH):
        need_ctx = i < DEPTH - 1
        even = i % 2 == 0
        j = i // 2
        mod = (s_lat @ ada_w[i] + ada_b[i])[:, None, :]
        sh1, sc1, gt1, sh2, sc2, gt2 = jnp.split(mod, N_MOD, axis=-1)
        h = modulate(rmsnorm(x, norm1_g[i]), sh1, sc1)
        if even or need_ctx:
            mod_c = (s_ctx @ ada_w[i] + ada_b[i])[None, None, :]
            csh1, csc1, cgt1, csh2, csc2, cgt2 = jnp.split(mod_c, N_MOD, axis=-1)
            hc = modulate(rmsnorm(ctx, norm1_g[i]), csh1, csc1)
        if even:
            y, yc = hybrid_ab_mixer(h, hc, mix_w_in[j], mix_w_out[j], attn_sink[j],
                                    shift_mu_prev[j], shift_mu_next[j], decay_w0[j], decay_w2[j],
                                    iclr_a0[j], iclr_a2[j], gate_g2[j], key_kk[j], key_ka[j],
                                    bonus_rk[j], lnx_g[j], lnx_b[j], ang_r, ang_c, need_ctx)
        else:
            y = fourier_mixer(h, fourier_w_out[j])
            yc = fourier_mixer(hc, fourier_w_out[j]) if need_ctx else None
        x = x + gt1 * y
        x = x + gt2 * channel_mlp(modulate(rmsnorm(x, norm2_g[i]), sh2, sc2), mlp_w1[i], mlp_w2[i])
        if need_ctx:
            ctx = ctx + cgt1 * yc
            ctx = ctx + cgt2 * channel_mlp(modulate(rmsnorm(ctx, norm2_g[i]), csh2, csc2),
                                           mlp_w1[i], mlp_w2[i])
    return rmsnorm(x, final_g)
```

```python
import numpy as np
from contextlib import ExitStack
import concourse.bass as bass
import concourse.mybir as mybir
from concourse.bass_utils import run_bass_kernel_spmd


F32 = mybir.dt.float32
BF16 = mybir.dt.bfloat16
AF = mybir.ActivationFunctionType
ALU = mybir.AluOpType
AX = mybir.AxisListType


class KB:
    ENG = ("pe", "act", "dve", "pool", "sp")

    def __init__(self, nc, n_dma_sems=6):
        self.nc = nc
        self.prog = {e: [] for e in self.ENG}
        self.cnt = {e: 0 for e in self.ENG}
        self.sem = {}
        self.waited = {e: {} for e in self.ENG}
        self.last_w = {}
        self.reads = {}
        self.n_dma_sems = n_dma_sems
        self.dma_sem_val = {}
        self.dma_rr = {q: 0 for q in ("sp", "act", "pool")}
        self.ninstr = 0

    def _need(self, eng, tok, out):
        if tok is None:
            return
        s, v = tok
        if eng == "pe" and s == "c_pe":
            return
        if self.waited[eng].get(s, 0) >= v:
            return
        out[s] = max(out.get(s, 0), v)

    def _deps(self, eng, reads, writes):
        need = {}
        for k in reads:
            self._need(eng, self.last_w.get(k), need)
        for k in writes:
            self._need(eng, self.last_w.get(k), need)
            for t in self.reads.get(k, ()):
                self._need(eng, t, need)
        for s, v in need.items():
            self.waited[eng][s] = v
        return list(need.items())

    def _commit(self, tok, reads, writes):
        for k in reads:
            self.reads.setdefault(k, []).append(tok)
        for k in writes:
            self.last_w[k] = tok
            self.reads[k] = []

    def op(self, eng, fn, reads=(), writes=()):
        waits = self._deps(eng, reads, writes)
        self.cnt[eng] += 1
        tok = ("c_" + eng, self.cnt[eng])
        self.prog[eng].append(("op", waits, fn, tok))
        self._commit(tok, reads, writes)
        self.ninstr += 1
        return tok

    def dma(self, q, out, in_, reads=(), writes=(), **kw):
        i = self.dma_rr[q]
        self.dma_rr[q] = (i + 1) % self.n_dma_sems
        sname = "d_%s_%d" % (q, i)
        prev = self.dma_sem_val.get(sname, 0)
        waits = dict(self._deps(q, reads, writes))
        if prev > 0 and self.waited[q].get(sname, 0) < prev:
            waits[sname] = prev
            self.waited[q][sname] = prev
        val = prev + 16
        self.dma_sem_val[sname] = val
        tok = (sname, val)
        self.prog[q].append(("dma", list(waits.items()), (out, in_, kw), tok))
        self._commit(tok, reads, writes)
        self.ninstr += 1
        return tok

    def finish_wait(self, eng, toks):
        need = {}
        for t in toks:
            self._need(eng, t, need)
        self.prog[eng].append(("wait", list(need.items()), None, None))

    def emit(self):
        nc = self.nc
        semnames = ["c_" + e for e in self.ENG if e != "sp"]
        for q in ("sp", "act", "pool"):
            for i in range(self.n_dma_sems):
                semnames.append("d_%s_%d" % (q, i))
        used = set()
        for e in self.ENG:
            for kind, waits, _, tok in self.prog[e]:
                for s, _v in waits:
                    used.add(s)
                if tok is not None:
                    used.add(tok[0])
        semnames = [s for s in semnames if s in used]
        from contextlib import ExitStack
        with ExitStack() as es:
            sems = {s: es.enter_context(nc.semaphore(s)) for s in semnames}
            block = es.enter_context(nc.Block())

            def run(engname):
                def body(engine):
                    for kind, waits, payload, tok in self.prog[engname]:
                        for s, v in waits:
                            engine.wait_ge(sems[s], v)
                        if kind == "op":
                            ins = payload(engine)
                            ins.then_inc(sems[tok[0]], 1)
                        elif kind == "dma":
                            out, in_, kw = payload
                            engine.dma_start(out=out, in_=in_, **kw).then_inc(sems[tok[0]], 16)
                return body

            if self.prog["sp"]:
                block.sync(run("sp"))
            if self.prog["pe"]:
                block.tensor(run("pe"))
            if self.prog["act"]:
                block.scalar(run("act"))
            if self.prog["dve"]:
                block.vector(run("dve"))
            if self.prog["pool"]:
                block.gpsimd(run("pool"))

    def mm(self, out, lhsT, rhs, start, stop, reads, writes):
        return self.op("pe", lambda e: e.matmul(out, lhsT=lhsT, rhs=rhs, start=start, stop=stop), reads, writes)

    def tr(self, out, in_, ident, reads, writes):
        return self.op("pe", lambda e: e.transpose(out=out, in_=in_, identity=ident), reads, writes)

    def tt(self, eng, out, in0, in1, op, reads, writes):
        return self.op(eng, lambda e: e.tensor_tensor(out=out, in0=in0, in1=in1, op=op), reads, writes)

    def ts(self, eng, out, in0, s1, s2, op0, op1, reads, writes):
        if op1 is None:
            return self.op(eng, lambda e: e.tensor_scalar(out=out, in0=in0, scalar1=s1, scalar2=None, op0=op0), reads, writes)
        return self.op(eng, lambda e: e.tensor_scalar(out=out, in0=in0, scalar1=s1, scalar2=s2, op0=op0, op1=op1), reads, writes)

    def stt(self, eng, out, in0, scalar, in1, op0, op1, reads, writes):
        return self.op(eng, lambda e: e.scalar_tensor_tensor(out=out, in0=in0, scalar=scalar, in1=in1, op0=op0, op1=op1), reads, writes)

    def actf(self, out, in_, func, reads, writes, **kw):
        return self.op("act", lambda e: e.activation(out=out, in_=in_, func=func, **kw), reads, writes)

    def cp(self, eng, out, in_, reads, writes):
        if eng == "act":
            return self.op("act", lambda e: e.copy(out=out, in_=in_), reads, writes)
        return self.op(eng, lambda e: e.tensor_copy(out=out, in_=in_), reads, writes)

    def memset(self, eng, ap, val, writes):
        return self.op(eng, lambda e: e.memset(ap, val), (), writes)

    def recip(self, out, in_, reads, writes):
        return self.op("dve", lambda e: e.reciprocal(out=out, in_=in_), reads, writes)

    def scan(self, out, d0, d1, initial, op0, op1, reads, writes):
        return self.op("dve", lambda e: e.tensor_tensor_scan(out=out, data0=d0, data1=d1, initial=initial, op0=op0, op1=op1), reads, writes)


NT = 34
NCOL = 3200
ZC0 = 1408
VC0 = 1280
XW = NT * 128
KW = XW + 256
EPS = 1e-6


def build_L1():
    nc = bass.Bass("TRN2", target_bir_lowering=False)
    D = lambda n, s, kind="ExternalInput": nc.dram_tensor(n, s, F32, kind=kind).ap()
    xh = D("xh", [XW, 1024]); ctx = D("ctx", [256, 1024]); cT = D("cT", [128, 8, 2])
    adaw = D("adaw", [1024, 2048]); adab = D("adab", [128, 16]); g1 = D("g1", [128, 8])
    win = D("win", [1024, NCOL]); ropec = D("ropec", [64, XW]); ropes = D("ropes", [64, XW])
    masks = D("masks", [128, 4 * 128]); sinkb = D("sinkb", [64, 8]); ident = D("ident", [128, 128])
    zr = D("zr", [14, 128, KW], "ExternalOutput"); att = D("att", [8, 64, 4096], "ExternalOutput")
    k = KB(nc)
    with ExitStack() as es:
        T = lambda name, shape, dt: es.enter_context(nc.sbuf_tensor(name, shape, dt))
        P = lambda name, shape, dt: es.enter_context(nc.psum_tensor(name, shape, dt))
        ids = T("ids", [128, 128], F32); idb = T("idb", [128, 128], BF16)
        cs = T("cs", [128, 8, 2], F32); sT = T("sT", [128, 8, 2], F32)
        adat = [T("adat%d" % i, [128, 8, 128], F32) for i in range(2)]
        adabs = T("adabs", [128, 16], F32); g1s = T("g1s", [128, 8], F32)
        modT = T("modT", [128, 16, 2], F32)
        Am = T("Am", [128, 8, 2], F32)
        wb = T("wb", [128, 8, NCOL], BF16)
        wst = [T("wst%d" % i, [128, 1600], F32) for i in range(2)]
        rc = T("rc", [64, XW], BF16); rs = T("rs", [64, XW], BF16)
        mkb = T("mkb", [128, 4, 128], BF16)
        esink = T("esink", [64, 8], F32)
        onesb = T("onesb", [128, 64], BF16)
        xt = [T("xt%d" % i, [128, 1024], F32) for i in range(2)]
        ss = [T("ss%d" % i, [128, 1], F32) for i in range(2)]
        rstd = [T("rstd%d" % i, [128, 1], F32) for i in range(2)]
        xn = [T("xn%d" % i, [128, 1024], BF16) for i in range(2)]
        htmp = T("htmp", [128, 8, 128], F32)
        hTg = [T("hTg%d" % i, [128, 8, 512], BF16) for i in range(2)]
        qTg = [T("qTg%d" % i, [128, 4, 512], BF16) for i in range(2)]
        qT8 = T("qT8", [128, 4, 256], BF16)
        kT = T("kT", [128, KW], BF16)
        Vtm = T("Vtm", [128, 36, 128], BF16)
        r1 = T("r1", [64, 512], F32); r2 = T("r2", [64, 512], F32)
        zs = [T("zs%d" % i, [128, 512], F32) for i in range(2)]
        NEB = 6
        Eb = [T("Eb%d" % i, [128, 512], BF16) for i in range(NEB)]
        dn = T("dn", [64, 4, 128], F32)
        atS = [T("atS%d" % i, [64, 4, 128], F32) for i in range(2)]
        pT = P("pT", [128, 8, 128], BF16)
        PA = P("PA", [128, 512], F32); PB = P("PB", [128, 512], F32)
        PZ = [P("PZ%d" % i, [128, 512], F32) for i in range(2)]
        PS = P("PS", [128, 512], F32)
        PN = P("PN", [64, 512], F32); PD = P("PD", [64, 512], F32)
        pmod = PA

        k.dma("sp", ids[:], ident, writes=["ids"])
        k.cp("dve", idb[:], ids[:], ["ids"], ["idb"])
        k.dma("sp", cs[:], cT, writes=["cs"])
        k.actf(sT[:], cs[:], AF.Silu, ["cs"], ["sT"])
        k.dma("sp", adabs[:], adab, writes=["adabs"])
        k.dma("sp", g1s[:], g1, writes=["g1s"])
        k.dma("act", wst[0][:, 0:512], masks, writes=["wst0"])
        k.cp("pool", mkb[:].rearrange("p a b -> p (a b)"), wst[0][:, 0:512], ["wst0"], ["mkb"])
        k.dma("act", esink[:], sinkb, writes=["esink0"])
        k.actf(esink[:], esink[:], AF.Exp, ["esink0"], ["esink"])
        k.memset("pool", onesb[:], 1.0, ["onesb"])
        ri = 0
        for (src, dst, dn_) in ((ropec, rc, "rc"), (ropes, rs, "rs")):
            for c0 in range(0, XW, 1600):
                n_ = min(1600, XW - c0)
                w = wst[ri % 2]; wn = "wst%d" % (ri % 2); ri += 1
                k.dma("act", w[0:64, 0:n_], src[:, c0:c0 + n_], writes=[wn])
                k.cp("pool", dst[:, c0:c0 + n_], w[0:64, 0:n_], [wn], [dn_])
        adv = adaw.rearrange("(kc p) n -> p kc n", p=128)
        for ch in range(16):
            a = adat[ch % 2]; an = "adat%d" % (ch % 2)
            k.dma("sp" if ch % 2 == 0 else "pool", a[:], adv[:, :, ch * 128:(ch + 1) * 128], writes=[an])
            for kc in range(8):
                k.mm(pmod[:, ch * 2:ch * 2 + 2], a[:, kc, :], sT[:, kc, :], kc == 0, kc == 7, [an, "sT"], ["PA"])
        k.tt("dve", modT[:], pmod[:, 0:32].rearrange("p (c j) -> p c j", j=2),
             adabs[:].unsqueeze(2).to_broadcast([128, 16, 2]), ALU.add, ["PA", "adabs"], ["modT"])
        k.stt("dve", Am[:], modT[:, 8:16, :], 1.0, g1s[:].unsqueeze(2).to_broadcast([128, 8, 2]), ALU.add, ALU.mult,
              ["modT", "g1s"], ["Am"])
        for kc in range(8):
            for hf in range(2):
                w = wst[ri % 2]; wn = "wst%d" % (ri % 2); ri += 1
                k.dma("sp" if hf == 0 else "pool", w[:], win[kc * 128:(kc + 1) * 128, hf * 1600:(hf + 1) * 1600], writes=[wn])
                k.cp("dve" if hf == 0 else "act", wb[:, kc, hf * 1600:(hf + 1) * 1600], w[:], [wn], ["wb%d" % kc])

        tile_ctr = [0]

        def do_tile(src_ap, j, hbuf, hname, col):
            i = tile_ctr[0] % 2; tile_ctr[0] += 1
            X = "xt%d" % i; N_ = "xn%d" % i; S_ = "ss%d" % i; R_ = "rstd%d" % i
            k.dma("sp", xt[i][:], src_ap, writes=[X])
            k.actf(xn[i][:], xt[i][:], AF.Square, [X], [N_, S_], accum_out=ss[i][:])
            k.ts("dve", rstd[i][:], ss[i][:], 1.0 / 1024, EPS, ALU.mult, ALU.add, [S_], [R_])
            k.actf(rstd[i][:], rstd[i][:], AF.Sqrt, [R_], [R_])
            k.recip(rstd[i][:], rstd[i][:], [R_], [R_])
            k.ts("pool", xn[i][:], xt[i][:], rstd[i][:, 0:1], None, ALU.mult, None, [X, R_], [N_])
            for c in range(8):
                k.tr(pT[:, c, :], xn[i][:, c * 128:(c + 1) * 128], idb[:], [N_, "idb"], ["pT"])
            k.tt("dve", htmp[:], pT[:], Am[:, :, j:j + 1].to_broadcast([128, 8, 128]), ALU.mult, ["pT", "Am"], ["htmp"])
            k.tt("pool", hbuf[:, :, col:col + 128], htmp[:], modT[:, 0:8, j:j + 1].to_broadcast([128, 8, 128]), ALU.add,
                 ["htmp", "modT"], [hname])

        groups = []
        groups.append(dict(tiles=[(ctx[0:128, :], 1), (ctx[128:256, :], 1)], n=256, kcol=XW, vt=34, rope=None, q=None))
        groups.append(dict(tiles=[(xh[32 * 128:33 * 128, :], 0), (xh[33 * 128:34 * 128, :], 0)], n=256, kcol=32 * 128, vt=32,
                           rope=32 * 128, q="q8"))
        for g in range(8):
            groups.append(dict(tiles=[(xh[(4 * g + t) * 128:(4 * g + t + 1) * 128, :], 0) for t in range(4)], n=512,
                               kcol=g * 512, vt=4 * g, rope=g * 512, q=g))
        zsr = [0]; ebr = [0]; att_ctr = [0]
        out_toks = []

        def attention(e_tile, qbuf, qname, qcol):
            for g in range(2):
                pr = slice(64 * g, 64 * g + 64)
                kbs = [((e_tile - 1) * 128, e_tile - 1, 0 if e_tile > 1 else 2), (e_tile * 128, e_tile, None),
                       ((e_tile + 1) * 128, e_tile + 1, 1 if e_tile < 32 else 3), (XW, 34, None), (XW + 128, 35, None)]
                ebs = []
                for (kc0, vt, mi) in kbs:
                    bi = ebr[0] % NEB; ebr[0] += 1
                    E = Eb[bi]; En = "Eb%d" % bi
                    k.mm(PS[:], kT[pr, kc0:kc0 + 128], qbuf[pr, :, qcol:qcol + 128], True, True,
                         ["kT%d" % (kc0 // 128), qname], ["PS"])
                    k.actf(E[:], PS[:], AF.Exp, ["PS"], [En], scale=0.125)
                    if mi is not None:
                        Ev = E[:].rearrange("p (h t) -> p h t", h=4)
                        k.tt("pool", Ev, Ev, mkb[:, mi:mi + 1, :].to_broadcast([128, 4, 128]), ALU.mult, [En, "mkb"], [En])
                    ebs.append((E, En, vt))
                for n_, (E, En, vt) in enumerate(ebs):
                    k.mm(PN[:], Vtm[:, vt, g * 64:(g + 1) * 64], E[:], n_ == 0, n_ == 4, [En, "V%d" % vt], ["PN"])
                for n_, (E, En, vt) in enumerate(ebs):
                    k.mm(PD[:], onesb[:], E[:], n_ == 0, n_ == 4, [En, "onesb"], ["PD"])
                ai = att_ctr[0] % 2; att_ctr[0] += 1
                k.tt("dve", dn[:], PD[:].rearrange("p (h t) -> p h t", h=4),
                     esink[:, 4 * g:4 * g + 4].unsqueeze(2).to_broadcast([64, 4, 128]), ALU.add, ["PD", "esink"], ["dn"])
                k.recip(dn[:], dn[:], ["dn"], ["dn"])
                k.tt("dve", atS[ai][:], PN[:].rearrange("p (h t) -> p h t", h=4), dn[:], ALU.mult, ["PN", "dn"], ["atS%d" % ai])
                c0 = (e_tile - 1) * 128
                out_toks.append(k.dma("act", att[4 * g:4 * g + 4, :, c0:c0 + 128].rearrange("h d t -> d h t"), atS[ai][:],
                                      reads=["atS%d" % ai], writes=["att_%d_%d" % (e_tile, g)]))

        def do_group(gi, G):
            hb = hTg[gi % 2]; hn = "hTg%d" % (gi % 2)
            n = G["n"]
            for t, (src, j) in enumerate(G["tiles"]):
                do_tile(src, j, hb, hn, t * 128)
            heads = []
            if G["q"] is not None:
                for hh in range(8):
                    heads.append(("q", hh, 128 * hh))
            for g in range(2):
                heads.append(("k", g, 1024 + 128 * g))
            qbuf = qname = None
            if G["q"] == "q8":
                qbuf, qname = qT8, "qT8"
            elif G["q"] is not None:
                qbuf, qname = qTg[G["q"] % 2], "qTg%d" % (G["q"] % 2)
            WBk = ["wb%d" % i for i in range(8)]
            kkeys = ["kT%d" % (G["kcol"] // 128 + t_) for t_ in range(n // 128)]
            for (kind, hh, c0) in heads:
                for kc in range(8):
                    k.mm(PA[0:64, 0:n], wb[:, kc, c0:c0 + 64], hb[:, kc, 0:n], kc == 0, kc == 7, [hn, "wb%d" % kc], ["PA"])
                if G["rope"] is None:
                    k.cp("act", kT[64 * hh:64 * hh + 64, G["kcol"]:G["kcol"] + n], PA[0:64, 0:n], ["PA"], kkeys)
                    continue
                for kc in range(8):
                    k.mm(PB[0:64, 0:n], wb[:, kc, c0 + 64:c0 + 128], hb[:, kc, 0:n], kc == 0, kc == 7, [hn, "wb%d" % kc], ["PB"])
                ro = G["rope"]
                k.tt("dve", r1[:, 0:n], PA[0:64, 0:n], rc[:, ro:ro + n], ALU.mult, ["PA", "rc"], ["r1"])
                k.tt("dve", r2[:, 0:n], PB[0:64, 0:n], rs[:, ro:ro + n], ALU.mult, ["PB", "rs"], ["r2"])
                if kind == "q":
                    g = hh // 4
                    k.tt("pool", qbuf[64 * g:64 * g + 64, hh % 4, 0:n], r1[:, 0:n], r2[:, 0:n], ALU.add, ["r1", "r2"], [qname])
                else:
                    k.tt("pool", kT[64 * hh:64 * hh + 64, G["kcol"]:G["kcol"] + n], r1[:, 0:n], r2[:, 0:n], ALU.add,
                         ["r1", "r2"], kkeys)
            for t in range(n // 128):
                pz = PZ[t % 2]; pzn = "PZ%d" % (t % 2)
                for kc in range(8):
                    k.mm(pz[:, 0:128], hb[:, kc, t * 128:(t + 1) * 128], wb[:, kc, VC0:VC0 + 128], kc == 0, kc == 7,
                         [hn, "wb%d" % kc], [pzn])
                vt = G["vt"] + t
                k.cp("act", Vtm[:, vt, :], pz[:, 0:128], [pzn], ["V%d" % vt])
            for ch in range(14):
                pz = PZ[ch % 2]; pzn = "PZ%d" % (ch % 2)
                for kc in range(8):
                    k.mm(pz[:, 0:n], wb[:, kc, ZC0 + ch * 128:ZC0 + (ch + 1) * 128], hb[:, kc, 0:n], kc == 0, kc == 7,
                         [hn, "wb%d" % kc], [pzn])
                zi = zsr[0] % 2; zsr[0] += 1
                k.cp("dve" if ch % 2 == 0 else "act", zs[zi][:, 0:n], pz[:, 0:n], [pzn], ["zs%d" % zi])
                out_toks.append(k.dma("sp", zr[ch, :, G["kcol"]:G["kcol"] + n], zs[zi][:, 0:n], reads=["zs%d" % zi],
                                      writes=["zr_%d_%d" % (ch, gi)]))
            if isinstance(G["q"], int):
                g8 = G["q"]
                if g8 >= 1:
                    attention(4 * g8 - 1, qTg[(g8 - 1) % 2], "qTg%d" % ((g8 - 1) % 2), 384)
                for t in range(3):
                    e_tile = 4 * g8 + t
                    if e_tile >= 1:
                        attention(e_tile, qTg[g8 % 2], "qTg%d" % (g8 % 2), t * 128)
                if g8 == 7:
                    attention(31, qTg[1], "qTg1", 384)
                    attention(32, qT8, "qT8", 0)

        for gi, G in enumerate(groups):
            do_group(gi, G)
        k.finish_wait("sp", out_toks)
        k.emit()
    print("L1 instrs", k.ninstr)
    return nc


def rope_tables(t_start):
    pos = np.arange(XW) + t_start
    pos = np.clip(pos, 0, 8191)
    row = (pos // 64).astype(np.float32); col = (pos % 64).astype(np.float32)
    inv = (10000.0 ** (-np.arange(0, 32, 2, dtype=np.float32) / 32)).astype(np.float32)
    ar = row[None, :] * inv[:, None]; ac = col[None, :] * inv[:, None]
    cosf = np.concatenate([np.cos(ar), np.cos(ar), np.cos(ac), np.cos(ac)], 0)
    sinf = np.concatenate([-np.sin(ar), np.sin(ar), -np.sin(ac), np.sin(ac)], 0)
    return cosf.astype(np.float32), sinf.astype(np.float32)


def perm64():
    p = np.zeros(64, np.int64)
    for base in (0, 32):
        for i in range(16):
            p[base + i] = base + i + 16
            p[base + 16 + i] = base + i
    return p


def l1_inputs(inp, b, s):
    t0 = s * 4096
    x = inp["x"][b]
    xh = np.zeros((XW, 1024), np.float32)
    lo = t0 - 128; hi = t0 + 4096 + 128
    a = max(lo, 0); bb = min(hi, 8192)
    xh[a - lo:bb - lo] = x[a:bb]
    cT = np.stack([inp["c"][b], inp["c_ctx"]], -1).reshape(8, 128, 2).transpose(1, 0, 2)
    w = inp["mix_w_in"][0]
    p = perm64()
    cols = []
    for hh in range(8):
        cols.append(w[:, hh * 64:(hh + 1) * 64]); cols.append(w[:, hh * 64 + p])
    for g in range(2):
        cols.append(w[:, 512 + g * 64:512 + (g + 1) * 64]); cols.append(w[:, 512 + g * 64 + p])
    cols.append(w[:, 640:768]); cols.append(w[:, 768:2560])
    win = np.concatenate(cols, 1)
    assert win.shape[1] == NCOL
    rcos, rsin = rope_tables(lo)
    j = np.arange(128)[:, None]; a_ = np.arange(128)[None, :]
    mp = (j >= a_).astype(np.float32); mn = (j <= a_).astype(np.float32)
    mp_first = mp if s == 1 else np.zeros_like(mp)
    mn_last = mn if s == 0 else np.zeros_like(mn)
    masks = np.stack([mp, mn, mp_first, mn_last], 1).reshape(128, 512)
    return {
        "xh": xh, "ctx": np.ascontiguousarray(inp["ctx"][b]), "cT": np.ascontiguousarray(cT),
        "adaw": np.ascontiguousarray(inp["ada_w"][0][:, :2048]),
        "adab": np.ascontiguousarray(inp["ada_b"][0][:2048].reshape(16, 128).T),
        "g1": np.ascontiguousarray(inp["norm1_g"][0].reshape(8, 128).T),
        "win": np.ascontiguousarray(win), "ropec": rcos, "ropes": rsin, "masks": np.ascontiguousarray(masks),
        "sinkb": np.ascontiguousarray(np.broadcast_to(inp["attn_sink"][0][None, :], (64, 8))).astype(np.float32),
        "ident": np.eye(128, dtype=np.float32),
    }


SEQ2 = 8448
NSB = 17
LOGC = -0.6065306597126334


def build_L2(nsb=NSB, stage=9):
    nc = bass.Bass("TRN2", target_bir_lowering=False)
    D = lambda n, s, kind="ExternalInput": nc.dram_tensor(n, s, F32, kind=kind).ap()
    zseq = D("zseq", [14, 128, SEQ2])
    vecs = D("vecs", [128, 44])
    w2 = D("w2", [64, 512]); a2 = D("a2", [64, 512])
    cmask = D("cmask", [128, 256 + 256 + 128 + 128 + 512])
    bones = D("bones", [128, 128])
    yout = D("y", [8192, 512], "ExternalOutput")
    k = KB(nc)
    with ExitStack() as es:
        T = lambda name, shape, dt: es.enter_context(nc.sbuf_tensor(name, shape, dt))
        P = lambda name, shape, dt: es.enter_context(nc.psum_tensor(name, shape, dt))
        vs = T("vs", [128, 44], F32); omm = T("omm", [128, 14], F32)
        w2s = T("w2s", [64, 512], F32); a2s = T("a2s", [128, 512], F32)
        w2b = T("w2b", [64, 512], BF16); a2b = T("a2b", [128, 512], BF16)
        cm = T("cm", [128, 1280], F32)
        m1 = T("m1", [128, 256], BF16); m2 = T("m2", [128, 256], BF16); msl = T("msl", [128, 128], BF16)
        idb = T("idb", [128, 128], BF16); bonb = T("bonb", [128, 128], BF16)
        bon = T("bon", [128, 128], F32)
        omka = T("omka", [128, 4], F32)
        zin = [T("zin%d" % i, [128, 514], F32) for i in range(3)]
        zt = T("zt", [128, 512], F32)
        zsh = {n: [T("zs_%s%d" % (n, i), [128, 512], F32) for i in range(2)] for n in ("r", "k", "v")}
        zwa = T("zwa", [128, 512], F32)
        thb = T("thb", [64, 512], BF16); alb = T("alb", [128, 512], BF16)
        f32t = {n: T("f_" + n, [128, 512], F32) for n in ("lw", "a", "t", "kd", "kk", "rn", "kap", "b", "cum", "eq", "ein", "eout", "cl")}
        sqb = T("sqb", [128, 512], BF16)
        pc = [T("pc%d" % i, [128, 4], F32) for i in range(8)]
        npc = [T("npc%d" % i, [128, 4], F32) for i in range(8)]
        KR = [[T("KR%d_%d" % (i, c), [128, 2, 512], BF16) for c in range(4)] for i in range(2)]
        KDo = [[T("KDo%d_%d" % (i, c), [128, 512], BF16) for c in range(4)] for i in range(2)]
        BTo = [[T("BTo%d_%d" % (i, c), [128, 512], BF16) for c in range(4)] for i in range(2)]
        KDe = [[T("KDe%d_%d" % (i, c), [128, 512], BF16) for c in range(4)] for i in range(2)]
        BNe = [[T("BNe%d_%d" % (i, c), [128, 512], BF16) for c in range(4)] for i in range(2)]
        VB = [[T("VB%d_%d" % (i, c), [128, 512], BF16) for c in range(4)] for i in range(2)]
        AB1 = [T("AB1_%d" % h, [128, 256], BF16) for h in range(8)]
        NB = [T("NB_%d" % h, [128, 256], BF16) for h in range(8)]
        NTt = [T("NT_%d" % h, [128, 128], BF16) for h in range(8)]
        PX = [[T("PX%d_%d" % (h, i), [128, 256], BF16) for i in range(2)] for h in range(8)]
        PTt = [[T("PT%d_%d" % (h, i), [128, 128], BF16) for i in range(2)] for h in range(8)]
        W0T = [T("W0T_%d" % h, [128, 64], BF16) for h in range(8)]
        Ub = [T("Ub_%d" % h, [128, 64], BF16) for h in range(8)]
        KdTp = [T("KdTp_%d" % c, [128, 4, 64], BF16) for c in range(4)]
        BnTp = [T("BnTp_%d" % c, [128, 4, 64], BF16) for c in range(4)]
        Vtm = [T("Vtm_%d" % c, [128, 128], BF16) for c in range(4)]
        ST = T("ST", [128, 4, 64], F32); STb = T("STb", [128, 4, 64], BF16)
        Yb = [T("Yb%d" % i, [128, 512], F32) for i in range(2)]
        B0 = [P("B0_%d" % i, [128, 512], F32) for i in range(2)]
        LV = [P("LV%d" % i, [128, 512], F32) for i in range(2)]
        WB_ = [P("WB%d" % i, [128, 512], F32) for i in range(2)]
        TRp = P("TRp", [128, 8, 128], BF16)
        PP = P("PP", [128, 512], F32)

        k.dma("sp", vs[:], vecs, writes=["vs"])
        k.dma("sp", w2s[:], w2, writes=["w2s"])
        k.dma("sp", a2s[64:128, :], a2, writes=["a2s"])
        k.cp("dve", w2b[:], w2s[:], ["w2s"], ["w2b"])
        k.cp("dve", a2b[64:128, :], a2s[64:128, :], ["a2s"], ["a2b"])
        k.dma("act", cm[:], cmask, writes=["cm"])
        k.cp("pool", m1[:], cm[:, 0:256], ["cm"], ["m1"])
        k.cp("pool", m2[:], cm[:, 256:512], ["cm"], ["m2"])
        k.cp("pool", msl[:], cm[:, 512:640], ["cm"], ["msl"])
        k.cp("pool", idb[:], cm[:, 640:768], ["cm"], ["idb"])
        rmask = cm[:, 768:1280]
        k.dma("act", bon[:], bones, writes=["bon"])
        k.cp("pool", bonb[:], bon[:], ["bon"], ["bonb"])
        MUA = lambda c: vs[:, c:c + 1]
        MUB = lambda c: vs[:, 14 + c:15 + c]
        OMM = lambda c: omm[:, c:c + 1]
        W0 = lambda c: vs[:, 28 + c:29 + c]
        A0 = lambda c: vs[:, 32 + c:33 + c]
        KKW = lambda c: vs[:, 36 + c:37 + c]
        KA = lambda c: vs[:, 40 + c:41 + c]
        k.ts("dve", omka[:], vs[:, 40:44], -1.0, 1.0, ALU.mult, ALU.add, ["vs"], ["omka"])
        k.tt("dve", omm[:], vs[:, 0:14], vs[:, 14:28], ALU.add, ["vs"], ["vs"])
        k.ts("dve", omm[:], omm[:], -1.0, 1.0, ALU.mult, ALU.add, ["vs"], ["vs"])
        k.memset("pool", ST[:], 0.0, ["ST"])
        k.memset("pool", STb[:], 0.0, ["STb"])
        for c in range(4):
            k.memset("pool", KdTp[c][:], 0.0, ["KdTp%d" % c])
            k.memset("pool", BnTp[c][:], 0.0, ["BnTp%d" % c])

        zctr = [0]
        out_toks = []

        def load_shift(cidx, t0, n, seg_first, seg_last, dst, dname):
            zi = zctr[0] % 3; zctr[0] += 1
            Z = zin[zi]; Zn = "zin%d" % zi
            lo = t0 - 1 if not seg_first else t0
            hi = t0 + n + 1 if not seg_last else t0 + n
            if seg_first:
                k.memset("pool", Z[:, 0:1], 0.0, [Zn])
            if seg_last:
                k.memset("pool", Z[:, n + 1:n + 2], 0.0, [Zn])
            k.dma("sp" if zi % 2 == 0 else "act", Z[:, (lo - (t0 - 1)):(hi - (t0 - 1))], zseq[cidx, :, lo:hi], writes=[Zn])
            k.ts("pool", zt[:, 0:n], Z[:, 1:n + 1], OMM(cidx), None, ALU.mult, None, [Zn, "vs"], ["zt"])
            k.stt("dve", zt[:, 0:n], Z[:, 0:n], MUA(cidx), zt[:, 0:n], ALU.mult, ALU.add, [Zn, "zt", "vs"], ["zt"])
            k.stt("dve", dst[:, 0:n], Z[:, 2:n + 2], MUB(cidx), zt[:, 0:n], ALU.mult, ALU.add, [Zn, "zt", "vs"], [dname])

        def prep_sb(sb):
            par = sb % 2
            if sb == 0:
                t0, n = 0, 256
            else:
                t0, n = 256 + (sb - 1) * 512, 512
            seg_first = sb in (0, 1); seg_last = sb in (0, nsb_last)
            nch = n // 128
            F = f32t
            load_shift(12, t0, n, seg_first, seg_last, zwa, "zwa")
            if stage < 0.15:
                return t0, n
            k.actf(thb[:, 0:n], zwa[0:64, 0:n], AF.Tanh, ["zwa"], ["thb"])
            k.cp("pool", alb[64:128, 0:n], zwa[64:128, 0:n], ["zwa"], ["alb"])
            for c in range(4):
                i2 = (sb * 4 + c) % 2
                zr_, zk_, zv_ = zsh["r"][i2], zsh["k"][i2], zsh["v"][i2]
                rn_, kn_, vn_ = "zs_r%d" % i2, "zs_k%d" % i2, "zs_v%d" % i2
                load_shift(c, t0, n, seg_first, seg_last, zr_, rn_)
                load_shift(4 + c, t0, n, seg_first, seg_last, zk_, kn_)
                load_shift(8 + c, t0, n, seg_first, seg_last, zv_, vn_)
                if stage < 0.25:
                    continue
                k.mm(PP[:, 0:n], w2b[:, c * 128:(c + 1) * 128], thb[:, 0:n], True, True, ["w2b", "thb"], ["PP"])
                k.actf(F["lw"][:, 0:n], PP[:, 0:n], AF.Sigmoid, ["PP", "vs"], ["f_lw"], bias=W0(c))
                k.ts("pool", F["lw"][:, 0:n], F["lw"][:, 0:n], LOGC, None, ALU.mult, None, ["f_lw"], ["f_lw"])
                k.mm(PP[:, 0:n], a2b[64:128, c * 128:(c + 1) * 128], alb[64:128, 0:n], True, True, ["a2b", "alb"], ["PP"])
                k.actf(F["a"][:, 0:n], PP[:, 0:n], AF.Sigmoid, ["PP", "vs"], ["f_a"], bias=A0(c))
                k.ts("dve", F["t"][:, 0:n], F["a"][:, 0:n], KA(c), omka[:, c:c + 1], ALU.mult, ALU.add, ["f_a", "vs", "omka"], ["f_t"])
                k.tt("pool", F["kd"][:, 0:n], zk_[:, 0:n], F["t"][:, 0:n], ALU.mult, [kn_, "f_t"], ["f_kd"])
                if stage < 0.35:
                    continue
                k.ts("dve", F["kk"][:, 0:n], zk_[:, 0:n], KKW(c), None, ALU.mult, None, [kn_, "vs"], ["f_kk"])
                k.tt("pool", sqb[:, 0:n], F["kk"][:, 0:n], F["kk"][:, 0:n], ALU.mult, ["f_kk"], ["sqb"])
                k.mm(PP[:, 0:n], bonb[:], sqb[:, 0:n], True, True, ["bonb", "sqb"], ["PP"])
                k.actf(F["rn"][:, 0:n], PP[:, 0:n], AF.Sqrt, ["PP"], ["f_rn"], bias=1e-12)
                k.recip(F["rn"][:, 0:n], F["rn"][:, 0:n], ["f_rn"], ["f_rn"])
                k.tt("pool", F["kap"][:, 0:n], F["kk"][:, 0:n], F["rn"][:, 0:n], ALU.mult, ["f_kk", "f_rn"], ["f_kap"])
                k.tt("pool", F["b"][:, 0:n], F["kap"][:, 0:n], F["a"][:, 0:n], ALU.mult, ["f_kap", "f_a"], ["f_b"])
                if stage < 0.45:
                    continue
                k.scan(F["cum"][:, 0:n], rmask[:, 0:n], F["lw"][:, 0:n], 0.0, ALU.mult, ALU.add, ["cm", "f_lw"], ["f_cum"])
                pi = (sb * 4 + c) % 8
                PC = pc[pi]; NPC = npc[pi]
                cumv = F["cum"][:, 0:n].rearrange("p (a b) -> p a b", b=128)
                k.actf(PC[:, 0:nch], cumv[:, :, 127], AF.Exp, ["f_cum"], ["pc%d" % pi])
                k.ts("pool", NPC[:, 0:nch], PC[:, 0:nch], -1.0, None, ALU.mult, None, ["pc%d" % pi], ["npc%d" % pi])
                k.actf(F["eq"][:, 0:n], F["cum"][:, 0:n], AF.Exp, ["f_cum"], ["f_eq"])
                k.actf(F["eout"][:, 0:n], F["cum"][:, 0:n], AF.Exp, ["f_cum"], ["f_eout"], scale=-1.0)
                k.tt("dve", F["cl"][:, 0:n], F["cum"][:, 0:n], F["lw"][:, 0:n], ALU.subtract, ["f_cum", "f_lw"], ["f_cl"])
                k.actf(F["ein"][:, 0:n], F["cl"][:, 0:n], AF.Exp, ["f_cl"], ["f_ein"])
                if stage < 0.55:
                    continue
                kr = KR[par][c]; krn = "KR%d_%d" % (par, c)
                k.tt("dve", kr[:, 0, 0:n], F["kap"][:, 0:n], F["ein"][:, 0:n], ALU.mult, ["f_kap", "f_ein"], [krn])
                k.tt("pool", kr[:, 1, 0:n], zr_[:, 0:n], F["eq"][:, 0:n], ALU.mult, [rn_, "f_eq"], [krn])
                kdo = KDo[par][c]; kdon = "KDo%d_%d" % (par, c)
                bto = BTo[par][c]; bton = "BTo%d_%d" % (par, c)
                k.tt("dve", kdo[:, 0:n], F["kd"][:, 0:n], F["eout"][:, 0:n], ALU.mult, ["f_kd", "f_eout"], [kdon])
                k.tt("pool", bto[:, 0:n], F["b"][:, 0:n], F["eout"][:, 0:n], ALU.mult, ["f_b", "f_eout"], [bton])
                kde = KDe[par][c]; kden = "KDe%d_%d" % (par, c)
                bne = BNe[par][c]; bnen = "BNe%d_%d" % (par, c)
                k.tt("dve", kde[:, 0:n].rearrange("p (a b) -> p a b", b=128), kdo[:, 0:n].rearrange("p (a b) -> p a b", b=128),
                     PC[:, 0:nch].unsqueeze(2).to_broadcast([128, nch, 128]), ALU.mult, [kdon, "pc%d" % pi], [kden])
                k.tt("pool", bne[:, 0:n].rearrange("p (a b) -> p a b", b=128), bto[:, 0:n].rearrange("p (a b) -> p a b", b=128),
                     NPC[:, 0:nch].unsqueeze(2).to_broadcast([128, nch, 128]), ALU.mult, [bton, "npc%d" % pi], [bnen])
                vb = VB[par][c]; vbn = "VB%d_%d" % (par, c)
                k.cp("act", vb[:, 0:n], zv_[:, 0:n], [vn_], [vbn])
            return t0, n

        def chunk(sb, ci, t0):
            par = sb % 2
            cs = slice(ci * 128, ci * 128 + 128)
            is_ctx = (sb == 0)
            yb = Yb[(sb * 4 + ci) % 2]; ybn = "Yb%d" % ((sb * 4 + ci) % 2)
            for c2 in range(2):
                for cc in range(2):
                    c = 2 * c2 + cc
                    k.tr(TRp[:, 3 * cc + 0, :], KDe[par][c][:, cs], idb[:], ["KDe%d_%d" % (par, c), "idb"], ["TRp"])
                    k.tr(TRp[:, 3 * cc + 1, :], BNe[par][c][:, cs], idb[:], ["BNe%d_%d" % (par, c), "idb"], ["TRp"])
                    k.tr(TRp[:, 3 * cc + 2, :], VB[par][c][:, cs], idb[:], ["VB%d_%d" % (par, c), "idb"], ["TRp"])
                for cc in range(2):
                    c = 2 * c2 + cc
                    k.cp("dve", KdTp[c][:, 0, :], TRp[:, 3 * cc + 0, 0:64], ["TRp"], ["KdTp%d" % c])
                    k.cp("dve", KdTp[c][:, 3, :], TRp[:, 3 * cc + 0, 64:128], ["TRp"], ["KdTp%d" % c])
                    k.cp("dve", BnTp[c][:, 0, :], TRp[:, 3 * cc + 1, 0:64], ["TRp"], ["BnTp%d" % c])
                    k.cp("dve", BnTp[c][:, 3, :], TRp[:, 3 * cc + 1, 64:128], ["TRp"], ["BnTp%d" % c])
                    k.cp("dve", Vtm[c][:], TRp[:, 3 * cc + 2, :], ["TRp"], ["Vtm%d" % c])
            for h in range(8):
                if stage < 3:
                    break
                c = h // 2; hp = h % 2; pr = slice(64 * hp, 64 * hp + 64)
                b0 = B0[h % 2]; b0n = "B0_%d" % (h % 2)
                kr = KR[par][c]; krn = "KR%d_%d" % (par, c)
                kdo = KDo[par][c]; kdon = "KDo%d_%d" % (par, c)
                bto = BTo[par][c]; bton = "BTo%d_%d" % (par, c)
                k.mm(b0[:, 0:256], kdo[pr, cs], kr[pr, :, cs], True, True, [kdon, krn], [b0n])
                k.mm(b0[:, 256:512], bto[pr, cs], kr[pr, :, cs], True, True, [bton, krn], [b0n])
                lv = LV[0]
                k.mm(lv[:, 0:128], kr[pr, 0, cs], bto[pr, cs], True, True, [bton, krn], ["LV0"])
                k.tt("dve", AB1[h][:], b0[:, 0:256], m1[:], ALU.mult, [b0n, "m1"], ["AB1_%d" % h])
                k.tt("dve", NB[h][:], b0[:, 256:512], m2[:], ALU.mult, [b0n, "m2"], ["NB_%d" % h])
                k.tt("dve", NTt[h][:], lv[:, 0:128], msl[:], ALU.mult, ["LV0", "msl"], ["NT_%d" % h])
                px = PX[h][0]; pxn = "PX%d_0" % h
                k.tt("pool", px[:, 128:256], idb[:], NB[h][:, 0:128], ALU.subtract, ["idb", "NB_%d" % h], [pxn + "x"])
                if stage < 4:
                    continue
                k.mm(lv[:, 128:256], NTt[h][:], NB[h][:, 0:128], True, True, ["NT_%d" % h, "NB_%d" % h], ["LV0"])
                k.mm(lv[:, 256:384], NB[h][:, 0:128], NTt[h][:], True, True, ["NT_%d" % h, "NB_%d" % h], ["LV0"])
                k.cp("dve", px[:, 0:128], lv[:, 128:256], ["LV0"], [pxn + "p"])
                pt = PTt[h][0]; ptn = "PT%d_0" % h
                k.cp("dve", pt[:], lv[:, 256:384], ["LV0"], [ptn])
                cur = 0
                for m in range(1, 7):
                    lv = LV[m % 2]; lvn = "LV%d" % (m % 2)
                    px = PX[h][cur]; pxn = "PX%d_%d" % (h, cur)
                    pt = PTt[h][cur]; ptn = "PT%d_%d" % (h, cur)
                    nx = PX[h][1 - cur]; nxn = "PX%d_%d" % (h, 1 - cur)
                    nt = PTt[h][1 - cur]; ntn = "PT%d_%d" % (h, 1 - cur)
                    if m <= 4:
                        k.mm(lv[:, 0:256], pt[:], px[:, 0:256], True, True, [ptn, pxn + "p", pxn + "x"], [lvn])
                        k.mm(lv[:, 256:384], px[:, 0:128], pt[:], True, True, [ptn, pxn + "p"], [lvn])
                        k.cp("dve", nx[:, 0:128], lv[:, 0:128], [lvn], [nxn + "p"])
                        k.tt("dve", nx[:, 128:256], lv[:, 128:256], px[:, 128:256], ALU.add, [lvn, pxn + "x"], [nxn + "x"])
                        k.cp("dve", nt[:], lv[:, 256:384], [lvn], [ntn])
                    elif m == 5:
                        k.mm(lv[:, 128:256], pt[:], px[:, 128:256], True, True, [ptn, pxn + "x"], [lvn])
                        k.mm(lv[:, 256:384], px[:, 0:128], pt[:], True, True, [ptn, pxn + "p"], [lvn])
                        k.tt("dve", nx[:, 128:256], lv[:, 128:256], px[:, 128:256], ALU.add, [lvn, pxn + "x"], [nxn + "x"])
                        k.cp("dve", nt[:], lv[:, 256:384], [lvn], [ntn])
                    else:
                        k.mm(lv[:, 128:256], pt[:], px[:, 128:256], True, True, [ptn, pxn + "x"], [lvn])
                        k.tt("dve", nx[:, 128:256], lv[:, 128:256], px[:, 128:256], ALU.add, [lvn, pxn + "x"], [nxn + "x"])
                    cur = 1 - cur
                Tm = PX[h][cur][:, 128:256]; Tn = "PX%d_%dx" % (h, cur)
                if stage < 5:
                    continue
                wb_ = WB_[h % 2]; wbn = "WB%d" % (h % 2)
                k.mm(wb_[:, 0:64], kr[pr, 0, cs], STb[pr, c, :], True, False, [krn, "STb%d" % c], [wbn])
                k.mm(wb_[:, 0:64], AB1[h][:, 0:128], Vtm[c][:, 64 * hp:64 * hp + 64], False, True, ["AB1_%d" % h, "Vtm%d" % c], [wbn])
                k.cp("dve", W0T[h][:], wb_[:, 0:64], [wbn], ["W0T_%d" % h])
                k.mm(wb_[:, 64:128], Tm, W0T[h][:], True, True, [Tn, "W0T_%d" % h], [wbn])
                k.cp("dve", Ub[h][:], wb_[:, 64:128], [wbn], ["Ub_%d" % h])
                if stage < 6:
                    continue
                if not is_ctx:
                    k.mm(wb_[:, 128:192], kr[pr, 1, cs], STb[pr, c, :], True, False, [krn, "STb%d" % c], [wbn])
                    k.mm(wb_[:, 128:192], AB1[h][:, 128:256], Vtm[c][:, 64 * hp:64 * hp + 64], False, False,
                         ["AB1_%d" % h, "Vtm%d" % c], [wbn])
                    k.mm(wb_[:, 128:192], NB[h][:, 128:256], Ub[h][:], False, True, ["NB_%d" % h, "Ub_%d" % h], [wbn])
                    k.cp("dve", yb[:, 64 * h:64 * h + 64], wb_[:, 128:192], [wbn], [ybn])
                if stage < 7:
                    continue
                wbs = PP; wbsn = "PP"
                if hp == 1:
                    for hq in range(2):
                        hh = 2 * c + hq
                        k.mm(wbs[:, 192:256], KdTp[c][:, 2 * hq:2 * hq + 2, :].rearrange("p a b -> p (a b)"),
                             Vtm[c][:, 64 * hq:64 * hq + 64], hq == 0, False, ["KdTp%d" % c, "Vtm%d" % c], [wbsn])
                        k.mm(wbs[:, 192:256], BnTp[c][:, 2 * hq:2 * hq + 2, :].rearrange("p a b -> p (a b)"), Ub[hh][:], False, hq == 1,
                             ["BnTp%d" % c, "Ub_%d" % hh], [wbsn])
                if hp == 1:
                    pi = (sb * 4 + c) % 8
                    k.stt("dve", ST[:, c, :], ST[:, c, :], pc[pi][:, ci:ci + 1], wbs[:, 192:256], ALU.mult, ALU.add,
                          ["ST", wbsn, "pc%d" % pi], ["ST"])
                    k.cp("pool", STb[:, c, :], ST[:, c, :], ["ST"], ["STb%d" % c])
            if not is_ctx:
                tok0 = t0 + ci * 128 - 256
                out_toks.append(k.dma("sp", yout[tok0:tok0 + 128, :], yb[:], reads=[ybn], writes=["y%d" % tok0]))

        nsb_last = nsb - 1
        for sb in range(nsb):
            t0, n = prep_sb(sb)
            if stage >= 2:
                for ci in range(n // 128):
                    chunk(sb, ci, t0)
            elif stage < 1:
                k.cp("dve", f32t["t"][:, 0:n], zwa[:, 0:n], ["zwa"], ["f_t"])
                out_toks.append(k.dma("sp", yout[sb * 128:(sb + 1) * 128, 0:n], f32t["t"][:, 0:n], reads=["f_t"], writes=["dbg%d" % sb]))
            else:
                par = sb % 2
                for qi, (tl, tn) in enumerate(((KR[par][0][:, 0, :], "KR%d_0" % par), (KR[par][0][:, 1, :], "KR%d_0" % par),
                                               (KDo[par][0][:], "KDo%d_0" % par), (BTo[par][0][:], "BTo%d_0" % par),
                                               (KDe[par][0][:], "KDe%d_0" % par), (BNe[par][0][:], "BNe%d_0" % par),
                                               (VB[par][0][:], "VB%d_0" % par))):
                    k.cp("dve", f32t["t"][:, 0:n], tl[:, 0:n], [tn], ["f_t"])
                    out_toks.append(k.dma("sp", yout[(sb * 8 + qi) * 128:(sb * 8 + qi + 1) * 128, 0:n], f32t["t"][:, 0:n], reads=["f_t"], writes=["dbg%d_%d" % (sb, qi)]))
        k.finish_wait("sp", out_toks)
        k.emit()
    print("L2 instrs", k.ninstr)
    return nc


def l2_consts():
    j = np.arange(128)[:, None]; t = np.arange(128)[None, :]
    su = (j < t).astype(np.float32); iu = (j <= t).astype(np.float32); sl = (t < j).astype(np.float32)
    rm = np.ones((128, 512), np.float32); rm[:, 0::128] = 0
    cmask = np.concatenate([su, iu, su, -iu, sl, np.eye(128, dtype=np.float32), rm], 1)
    bones = np.zeros((128, 128), np.float32); bones[:64, :64] = 1; bones[64:, 64:] = 1
    return cmask, bones


def l2_inputs(inp, zfull, zc, d):
    if d == 0:
        seq = np.concatenate([zc, zfull], 0)
        mua, mub = inp["shift_mu_prev"][0], inp["shift_mu_next"][0]
    else:
        seq = np.concatenate([zc[::-1], zfull[::-1]], 0)
        mua, mub = inp["shift_mu_next"][0], inp["shift_mu_prev"][0]
    zseq = np.ascontiguousarray(seq.T.reshape(14, 128, SEQ2))
    col = lambda v: v.reshape(-1, 128).T
    vecs = np.concatenate([col(mua), col(mub), col(inp["decay_w0"][0, d]), col(inp["iclr_a0"][0, d]),
                           col(inp["key_kk"][0]), col(inp["key_ka"][0])], 1).astype(np.float32)
    cmask, bones = l2_consts()
    return {"zseq": zseq, "vecs": np.ascontiguousarray(vecs), "w2": np.ascontiguousarray(inp["decay_w2"][0, d]),
            "a2": np.ascontiguousarray(inp["iclr_a2"][0, d]), "cmask": cmask, "bones": bones}


NTOK = 4096
GN_EPS = 64e-5


def build_L3():
    nc = bass.Bass("TRN2", target_bir_lowering=False)
    D = lambda n, s, kind="ExternalInput": nc.dram_tensor(n, s, F32, kind=kind).ap()
    xin = D("xin", [NTOK, 1024]); attT = D("attT", [8, 64, NTOK])
    yf = D("yf", [NTOK, 512]); yb = D("yb", [NTOK, 512])
    zsh = D("zsh", [13, 128, NTOK + 2])
    vecs = D("vecs", [128, 13 * 2 + 4])
    lnx = D("lnx", [128, 1024])
    g2 = D("g2", [128, 512]); wout = D("wout", [1024, 1024])
    cT = D("cT", [128, 8, 2]); adaw = D("adaw", [1024, 1024]); adabg = D("adabg", [128, 1024])
    cm = D("cm", [128, 256])
    xout = D("xout", [NTOK, 1024], "ExternalOutput")
    k = KB(nc)
    with ExitStack() as es:
        T = lambda name, shape, dt: es.enter_context(nc.sbuf_tensor(name, shape, dt))
        P = lambda name, shape, dt: es.enter_context(nc.psum_tensor(name, shape, dt))
        vs = T("vs", [128, 30], F32); omm = T("omm", [128, 13], F32)
        lnxs = T("lnxs", [128, 1024], F32)
        g2s = T("g2s", [128, 512], F32); g2b = T("g2b", [128, 512], BF16)
        cms = T("cms", [128, 256], F32); bonb = T("bonb", [128, 128], BF16); idb = T("idb", [128, 128], BF16)
        cs = T("cs", [128, 8, 2], F32); srep = T("srep", [128, 8, 128], F32)
        gtb = T("gtb", [128, 1024], F32)
        stg = [T("stg%d" % i, [128, 1024], F32) for i in range(2)]
        woa = T("woa", [64, 8, 1024], BF16); wor = T("wor", [128, 4, 1024], BF16)
        zin = [T("zin%d" % i, [128, 514], F32) for i in range(3)]
        zt = T("zt", [128, 512], F32)
        zr_ = T("zr_", [128, 512], F32); zk_ = T("zk_", [128, 512], F32); zv_ = T("zv_", [128, 512], F32); zg_ = T("zg_", [128, 512], F32)
        rkb = T("rkb", [128, 512], BF16); sgb = T("sgb", [128, 512], BF16)
        bonus = [T("bonus%d" % c, [128, 512], F32) for c in range(4)]
        gT = [T("gT%d" % c, [128, 512], F32) for c in range(4)]
        ast = T("ast", [64, 8, 512], F32); attb = T("attb", [64, 8, 512], BF16)
        yt = [T("yt%d" % i, [128, 512], F32) for i in range(2)]
        yt2 = [T("yt2_%d" % i, [128, 512], F32) for i in range(2)]
        st8 = T("st8", [128, 8], F32); st8b = T("st8b", [128, 8], F32)
        cen = T("cen", [128, 512], F32); sq = T("sq", [128, 512], F32)
        ynb = T("ynb", [128, 512], BF16)
        rwt = T("rwt", [128, 4, 128], F32)
        rwT = T("rwT", [128, 4, 512], BF16)
        xt = [T("xt%d" % i, [128, 1024], F32) for i in range(2)]
        tmp = T("tmp", [128, 512], F32)
        pT = P("pT", [128, 4, 128], BF16)
        PA = P("PA", [128, 512], F32)
        PG = [P("PG%d" % i, [128, 512], F32) for i in range(2)]
        PO = [P("PO%d" % i, [128, 512], F32) for i in range(2)]

        k.dma("sp", vs[:], vecs, writes=["vs"])
        k.tt("dve", omm[:], vs[:, 0:13], vs[:, 13:26], ALU.add, ["vs"], ["omm"])
        k.ts("dve", omm[:], omm[:], -1.0, 1.0, ALU.mult, ALU.add, ["omm"], ["omm"])
        k.dma("sp", lnxs[:], lnx, writes=["lnxs"])
        k.dma("sp", g2s[:], g2, writes=["g2s"]); k.cp("dve", g2b[:], g2s[:], ["g2s"], ["g2b"])
        k.dma("sp", cms[:], cm, writes=["cms"])
        k.cp("dve", bonb[:], cms[:, 0:128], ["cms"], ["bonb"]); k.cp("dve", idb[:], cms[:, 128:256], ["cms"], ["idb"])
        k.dma("sp", cs[:], cT, writes=["cs"])
        k.actf(srep[:], cs[:, :, 0:1].to_broadcast([128, 8, 128]), AF.Silu, ["cs"], ["srep"])
        k.dma("sp", gtb[:], adabg, writes=["gtb0"])
        adv = adaw.rearrange("(kc p) n -> p kc n", p=128)
        ri = 0
        for hf in range(2):
            for q in range(4):
                a = stg[ri % 2]; an = "stg%d" % (ri % 2); ri += 1
                av = a[:, 0:1024].rearrange("p (kc n) -> p kc n", kc=2)
                k.dma("sp" if q % 2 == 0 else "pool", av, adv[:, 2 * q:2 * q + 2, hf * 512:(hf + 1) * 512], writes=[an])
                for kk_ in range(2):
                    kc = 2 * q + kk_
                    k.mm(PA[:, 0:512], srep[:, kc, :], av[:, kk_, :], kc == 0, kc == 7, [an, "srep"], ["PA"])
            k.tt("dve", gtb[:, hf * 512:(hf + 1) * 512], PA[:, 0:512], gtb[:, hf * 512:(hf + 1) * 512], ALU.add, ["PA", "gtb0"], ["gtb"])
        wa = wout[0:512, :].rearrange("(h d) n -> d h n", d=64)
        for h in range(8):
            w = stg[ri % 2]; wn = "stg%d" % (ri % 2); ri += 1
            k.dma("sp" if h % 2 == 0 else "pool", w[0:64, :], wa[:, h, :], writes=[wn])
            k.cp("dve" if h % 2 == 0 else "act", woa[:, h, :], w[0:64, :], [wn], ["woa"])
        for c in range(4):
            w = stg[ri % 2]; wn = "stg%d" % (ri % 2); ri += 1
            k.dma("sp" if c % 2 == 0 else "pool", w[:], wout[512 + c * 128:512 + (c + 1) * 128, :], writes=[wn])
            k.cp("dve" if c % 2 == 0 else "act", wor[:, c, :], w[:], [wn], ["wor"])

        zctr = [0]; out_toks = []; tctr = [0]

        def load_shift(cidx, t0, dst, dname):
            zi = zctr[0] % 3; zctr[0] += 1
            Z = zin[zi]; Zn = "zin%d" % zi
            k.dma("sp" if zi % 2 == 0 else "act", Z[:], zsh[cidx, :, t0:t0 + 514], writes=[Zn])
            k.ts("pool", zt[:], Z[:, 1:513], omm[:, cidx:cidx + 1], None, ALU.mult, None, [Zn, "omm"], ["zt"])
            k.stt("dve", zt[:], Z[:, 0:512], vs[:, cidx:cidx + 1], zt[:], ALU.mult, ALU.add, [Zn, "zt", "vs"], ["zt"])
            k.stt("dve", dst[:], Z[:, 2:514], vs[:, 13 + cidx:14 + cidx], zt[:], ALU.mult, ALU.add, [Zn, "zt", "vs"], [dname])

        for g in range(NTOK // 512):
            t0 = g * 512
            load_shift(12, t0, zg_, "zg_")
            k.actf(sgb[:], zg_[:], AF.Sigmoid, ["zg_"], ["sgb"])
            for c in range(4):
                pg = PG[c % 2]; pgn = "PG%d" % (c % 2)
                k.mm(pg[:], g2b[:, c * 128:(c + 1) * 128], sgb[:], True, True, ["g2b", "sgb"], [pgn])
                k.cp("act", gT[c][:], pg[:], [pgn], ["gT%d" % c])
            for c in range(4):
                load_shift(c, t0, zr_, "zr_"); load_shift(4 + c, t0, zk_, "zk_"); load_shift(8 + c, t0, zv_, "zv_")
                k.stt("dve", rkb[:], zr_[:], vs[:, 26 + c:27 + c], zk_[:], ALU.mult, ALU.mult, ["zr_", "zk_", "vs"], ["rkb"])
                pg = PG[c % 2]; pgn = "PG%d" % (c % 2)
                k.mm(pg[:], bonb[:], rkb[:], True, True, ["bonb", "rkb"], [pgn])
                k.tt("dve", bonus[c][:], pg[:], zv_[:], ALU.mult, [pgn, "zv_"], ["bonus%d" % c])
            k.dma("sp", ast[:], attT[:, :, t0:t0 + 512].rearrange("h d t -> d h t"), writes=["ast"])
            k.cp("act", attb[:], ast[:], ["ast"], ["attb"])
            for t in range(4):
                r0 = t0 + t * 128
                i = tctr[0] % 2; tctr[0] += 1
                Y = yt[i]; Yn = "yt%d" % i; Y2 = yt2[i]; Y2n = "yt2_%d" % i
                k.dma("sp", Y[:], yf[r0:r0 + 128, :], writes=[Yn])
                k.dma("act", Y2[:], yb[r0:r0 + 128, :], writes=[Y2n])
                k.dma("pool", xt[i][:], xin[r0:r0 + 128, :], writes=["xt%d" % i])
                k.tt("pool", Y[:], Y[:], Y2[:], ALU.add, [Yn, Y2n], [Yn])
                Yv = Y[:].rearrange("p (h d) -> p h d", d=64)
                k.op("dve", lambda e, Yv=Yv: e.tensor_reduce(out=st8[:], in_=Yv, axis=AX.X, op=ALU.add), [Yn], ["st8"])
                k.ts("dve", st8[:], st8[:], 1.0 / 64, None, ALU.mult, None, ["st8"], ["st8"])
                k.tt("dve", cen[:].rearrange("p (h d) -> p h d", d=64), Yv, st8[:].unsqueeze(2).to_broadcast([128, 8, 64]),
                     ALU.subtract, [Yn, "st8"], ["cen"])
                k.tt("pool", sq[:], cen[:], cen[:], ALU.mult, ["cen"], ["sq"])
                k.op("dve", lambda e: e.tensor_reduce(out=st8b[:], in_=sq[:].rearrange("p (h d) -> p h d", d=64), axis=AX.X, op=ALU.add),
                     ["sq"], ["st8b"])
                k.ts("dve", st8b[:], st8b[:], 1.0 / 64, GN_EPS, ALU.mult, ALU.add, ["st8b"], ["st8b"])
                k.actf(st8b[:], st8b[:], AF.Sqrt, ["st8b"], ["st8b"])
                k.recip(st8b[:], st8b[:], ["st8b"], ["st8b"])
                k.tt("dve", cen[:].rearrange("p (h d) -> p h d", d=64), cen[:].rearrange("p (h d) -> p h d", d=64),
                     st8b[:].unsqueeze(2).to_broadcast([128, 8, 64]), ALU.mult, ["cen", "st8b"], ["cen"])
                k.tt("pool", cen[:], cen[:], lnxs[:, 0:512], ALU.mult, ["cen", "lnxs"], ["cen"])
                k.tt("pool", ynb[:], cen[:], lnxs[:, 512:1024], ALU.add, ["cen", "lnxs"], ["ynb"])
                for c in range(4):
                    k.tr(pT[:, c, :], ynb[:, c * 128:(c + 1) * 128], idb[:], ["ynb", "idb"], ["pT"])
                cs_ = slice(t * 128, t * 128 + 128)
                for c in range(4):
                    k.tt("dve", rwt[:, c, :], pT[:, c, :], bonus[c][:, cs_], ALU.add, ["pT", "bonus%d" % c], ["rwt"])
                    k.tt("pool", rwT[:, c, cs_], rwt[:, c, :], gT[c][:, cs_], ALU.mult, ["rwt", "gT%d" % c], ["rwT"])
                for hf in range(2):
                    po = PO[hf]; pon = "PO%d" % hf
                    ns = slice(hf * 512, hf * 512 + 512)
                    for h in range(8):
                        k.mm(po[:], attb[:, h, cs_], woa[:, h, ns], h == 0, False, ["attb", "woa"], [pon])
                    for c in range(4):
                        k.mm(po[:], rwT[:, c, cs_], wor[:, c, ns], False, c == 3, ["rwT", "wor"], [pon])
                    k.tt("dve", tmp[:], po[:], gtb[:, ns], ALU.mult, [pon, "gtb"], ["tmp"])
                    k.tt("pool", xt[i][:, ns], tmp[:], xt[i][:, ns], ALU.add, ["tmp", "xt%d" % i], ["xt%d" % i])
                out_toks.append(k.dma("act", xout[r0:r0 + 128, :], xt[i][:], reads=["xt%d" % i], writes=["xo%d" % r0]))
        k.finish_wait("sp", out_toks)
        k.emit()
    print("L3 instrs", k.ninstr)
    return nc


def l3_inputs(inp, b, s, x_half, attT, yf_half, yb_half, zr_core):
    col = lambda v: np.ascontiguousarray(v.reshape(-1, 128).T)
    sel = [0, 1, 2, 3, 4, 5, 6, 7, 8, 9, 10, 11, 13]
    zsh = np.ascontiguousarray(zr_core[sel][:, :, 127:127 + 4098])
    if s == 0:
        zsh[:, :, 0] = 0
    if s == 1:
        zsh[:, :, -1] = 0
    mua = col(inp["shift_mu_prev"][0])[:, sel]; mub = col(inp["shift_mu_next"][0])[:, sel]
    rk = col(inp["bonus_rk"][0].reshape(512))
    bones = np.zeros((128, 128), np.float32); bones[:64, :64] = 1; bones[64:, 64:] = 1
    return {
        "xin": np.ascontiguousarray(x_half), "attT": np.ascontiguousarray(attT), "yf": np.ascontiguousarray(yf_half),
        "yb": np.ascontiguousarray(yb_half), "zsh": zsh,
        "vecs": np.ascontiguousarray(np.concatenate([mua, mub, rk], 1).astype(np.float32)),
        "lnx": np.ascontiguousarray(np.broadcast_to(np.concatenate([inp["lnx_g"][0], inp["lnx_b"][0]])[None, :], (128, 1024))).astype(np.float32),
        "g2": np.ascontiguousarray(inp["gate_g2"][0]), "wout": np.ascontiguousarray(inp["mix_w_out"][0]),
        "cT": np.ascontiguousarray(np.stack([col(inp["c"][b])] * 2, -1)),
        "adaw": np.ascontiguousarray(inp["ada_w"][0][:, 2048:3072]),
        "adabg": np.ascontiguousarray(np.broadcast_to(inp["ada_b"][0][2048:3072][None, :], (128, 1024))),
        "cm": np.concatenate([bones, np.eye(128, dtype=np.float32)], 1),
    }


EPS = 1e-6
NTOK = 4096


def build_Lmlp(final):
    nc = bass.Bass("TRN2", target_bir_lowering=False)
    D = lambda n, s, kind="ExternalInput": nc.dram_tensor(n, s, F32, kind=kind).ap()
    xin = D("xin", [NTOK, 1024]); cT = D("cT", [128, 8, 2])
    adaw = D("adaw", [1024, 3072]); adab = D("adab", [128, 16]); adabg = D("adabg", [128, 1024])
    g2 = D("g2", [128, 8]); w1 = D("w1", [1024, 4096]); w2 = D("w2", [4096, 1024])
    ident = D("ident", [128, 128])
    if final:
        fg = D("fg", [128, 1024])
    xout = D("xout", [NTOK, 1024], "ExternalOutput")
    k = KB(nc)
    with ExitStack() as es:
        T = lambda name, shape, dt: es.enter_context(nc.sbuf_tensor(name, shape, dt))
        P = lambda name, shape, dt: es.enter_context(nc.psum_tensor(name, shape, dt))
        ids = T("ids", [128, 128], F32); idb = T("idb", [128, 128], BF16)
        cs = T("cs", [128, 8, 2], F32); sT = T("sT", [128, 8, 2], F32); srep = T("srep", [128, 8, 128], F32)
        adabs = T("adabs", [128, 16], F32); g2s = T("g2s", [128, 8], F32)
        modT = T("modT", [128, 16, 2], F32); Am = T("Am", [128, 8], F32)
        gtb = T("gtb", [128, 1024], F32)
        w1b = T("w1b", [128, 8, 4096], BF16); w2b = T("w2b", [128, 32, 1024], BF16)
        stg = [T("stg%d" % i, [128, 1024], F32) for i in range(2)]
        xres = [T("xres%d" % i, [128, 1024], F32) for i in range(2)]
        ss = [T("ss%d" % i, [128, 1], F32) for i in range(2)]
        rstd = [T("rstd%d" % i, [128, 1], F32) for i in range(2)]
        xn = [T("xn%d" % i, [128, 1024], BF16) for i in range(2)]
        htmp = T("htmp", [128, 8, 128], F32)
        hT = [T("hT%d" % i, [128, 8, 128], BF16) for i in range(2)]
        rl = [T("rl%d" % i, [128, 128], BF16) for i in range(2)]
        uT = T("uT", [128, 32, 128], BF16)
        tmp = T("tmp", [128, 512], F32)
        if final:
            fgs = T("fgs", [128, 1024], F32)
        pT = P("pT", [128, 8, 128], BF16)
        PA = P("PA", [128, 512], F32)
        PM1 = [P("PM1_%d" % i, [128, 512], F32) for i in range(2)]
        PM2 = [P("PM2_%d" % i, [128, 512], F32) for i in range(2)]

        k.dma("sp", ids[:], ident, writes=["ids"])
        k.cp("dve", idb[:], ids[:], ["ids"], ["idb"])
        k.dma("sp", cs[:], cT, writes=["cs"])
        k.actf(sT[:], cs[:], AF.Silu, ["cs"], ["sT"])
        k.actf(srep[:], cs[:, :, 0:1].to_broadcast([128, 8, 128]), AF.Silu, ["cs"], ["srep"])
        k.dma("sp", adabs[:], adab, writes=["adabs"])
        k.dma("sp", g2s[:], g2, writes=["g2s"])
        k.dma("sp", gtb[:], adabg, writes=["gtb0"])
        if final:
            k.dma("sp", fgs[:], fg, writes=["fgs"])
        adv = adaw.rearrange("(kc p) n -> p kc n", p=128)
        for ch in range(16):
            a = stg[ch % 2]; an = "stg%d" % (ch % 2)
            av = a[:, 0:1024].rearrange("p (kc n) -> p kc n", kc=8)
            k.dma("sp" if ch % 2 == 0 else "pool", av, adv[:, :, ch * 128:(ch + 1) * 128], writes=[an])
            for kc in range(8):
                k.mm(PA[:, 2 * ch:2 * ch + 2], av[:, kc, :], sT[:, kc, :], kc == 0, kc == 7, [an, "sT"], ["PA"])
        k.tt("dve", modT[:], PA[:, 0:32].rearrange("p (c j) -> p c j", j=2), adabs[:].unsqueeze(2).to_broadcast([128, 16, 2]), ALU.add, ["PA", "adabs"], ["modT"])
        k.stt("dve", Am[:], modT[:, 8:16, 0], 1.0, g2s[:], ALU.add, ALU.mult, ["modT", "g2s"], ["Am"])
        ri = 0
        for hf in range(2):
            for q in range(4):
                a = stg[ri % 2]; an = "stg%d" % (ri % 2); ri += 1
                av = a[:, 0:1024].rearrange("p (kc n) -> p kc n", kc=2)
                k.dma("sp" if q % 2 == 0 else "pool", av, adv[:, 2 * q:2 * q + 2, 2048 + hf * 512:2048 + (hf + 1) * 512], writes=[an])
                for kk_ in range(2):
                    kc = 2 * q + kk_
                    k.mm(PA[:, 0:512], srep[:, kc, :], av[:, kk_, :], kc == 0, kc == 7, [an, "srep", "modT", "Am"], ["PA"])
            k.tt("dve", gtb[:, hf * 512:(hf + 1) * 512], PA[:, 0:512], gtb[:, hf * 512:(hf + 1) * 512], ALU.add, ["PA", "gtb0"], ["gtb"])
        for kc in range(8):
            for hf in range(4):
                w = stg[ri % 2]; wn = "stg%d" % (ri % 2); ri += 1
                k.dma("sp" if hf % 2 == 0 else "pool", w[:], w1[kc * 128:(kc + 1) * 128, hf * 1024:(hf + 1) * 1024], writes=[wn])
                k.cp("dve" if hf % 2 == 0 else "act", w1b[:, kc, hf * 1024:(hf + 1) * 1024], w[:], [wn], ["w1b"])
        for cc in range(32):
            w = stg[ri % 2]; wn = "stg%d" % (ri % 2); ri += 1
            k.dma("sp" if cc % 2 == 0 else "pool", w[:], w2[cc * 128:(cc + 1) * 128, :], writes=[wn])
            k.cp("dve" if cc % 2 == 0 else "act", w2b[:, cc, :], w[:], [wn], ["w2b"])

        out_toks = []
        tctr = [0]

        def do_tile(src_ap, xr, xrn, hbuf, hname, col):
            i = tctr[0] % 2; tctr[0] += 1
            N_ = "xn%d" % i; S_ = "ss%d" % i; R_ = "rstd%d" % i
            k.dma("sp", xr, src_ap, writes=[xrn])
            k.actf(xn[i][:], xr, AF.Square, [xrn], [N_, S_], accum_out=ss[i][:])
            k.ts("dve", rstd[i][:], ss[i][:], 1.0 / 1024, EPS, ALU.mult, ALU.add, [S_], [R_])
            k.actf(rstd[i][:], rstd[i][:], AF.Sqrt, [R_], [R_])
            k.recip(rstd[i][:], rstd[i][:], [R_], [R_])
            k.ts("pool", xn[i][:], xr, rstd[i][:, 0:1], None, ALU.mult, None, [xrn, R_], [N_])
            for c in range(8):
                k.tr(pT[:, c, :], xn[i][:, c * 128:(c + 1) * 128], idb[:], [N_, "idb"], ["pT"])
            k.tt("dve", htmp[:], pT[:], Am[:].unsqueeze(2).to_broadcast([128, 8, 128]), ALU.mult, ["pT", "Am"], ["htmp"])
            k.tt("pool", hbuf[:, :, col:col + 128], htmp[:], modT[:, 0:8, 0:1].to_broadcast([128, 8, 128]), ALU.add,
                 ["htmp", "modT"], [hname])

        ngroups = NTOK // 128
        for g in range(ngroups):
            hb = hT[g % 2]; hn = "hT%d" % (g % 2)
            xr = xres[g % 2]; xrn = "xres%d" % (g % 2)
            do_tile(xin[g * 128:(g + 1) * 128, :], xr[:], xrn, hb, hn, 0)
            for cc in range(32):
                pm = PM1[cc % 2]; pmn = "PM1_%d" % (cc % 2)
                for kc in range(8):
                    k.mm(pm[:, 0:128], w1b[:, kc, cc * 128:(cc + 1) * 128], hb[:, kc, :], kc == 0, kc == 7, [hn, "w1b"], [pmn])
                r = rl[cc % 2]; rn_ = "rl%d" % (cc % 2)
                k.actf(r[:], pm[:, 0:128], AF.Relu, [pmn], [rn_])
                k.tt("pool", uT[:, cc, :], r[:], r[:], ALU.mult, [rn_], ["uT"])
            for hf in range(2):
                pm = PM2[hf]; pmn = "PM2_%d" % hf
                for cc in range(32):
                    k.mm(pm[:], uT[:, cc, :], w2b[:, cc, hf * 512:(hf + 1) * 512], cc == 0, cc == 31, ["uT", "w2b"], [pmn])
                k.tt("dve", tmp[:], pm[:], gtb[:, hf * 512:(hf + 1) * 512], ALU.mult, [pmn, "gtb"], ["tmp"])
                k.tt("pool", xr[:, hf * 512:(hf + 1) * 512], tmp[:], xr[:, hf * 512:(hf + 1) * 512], ALU.add, ["tmp", xrn], [xrn])
            if final:
                i = tctr[0] % 2; tctr[0] += 1
                N_ = "xn%d" % i; S_ = "ss%d" % i; R_ = "rstd%d" % i
                k.actf(xn[i][:], xr[:], AF.Square, [xrn], [N_, S_], accum_out=ss[i][:])
                k.ts("dve", rstd[i][:], ss[i][:], 1.0 / 1024, EPS, ALU.mult, ALU.add, [S_], [R_])
                k.actf(rstd[i][:], rstd[i][:], AF.Sqrt, [R_], [R_])
                k.recip(rstd[i][:], rstd[i][:], [R_], [R_])
                k.stt("dve", xr[:], xr[:], rstd[i][:, 0:1], fgs[:], ALU.mult, ALU.mult, [xrn, R_, "fgs"], [xrn])
            out_toks.append(k.dma("act", xout[g * 128:(g + 1) * 128, :], xr[:], reads=[xrn], writes=["xout%d" % g]))
        k.finish_wait("sp", out_toks)
        k.emit()
    print("Lmlp instrs", k.ninstr)
    return nc


def lm_inputs(inp, layer, b, x_tok, final):
    col = lambda v: np.ascontiguousarray(v.reshape(-1, 128).T)
    d = {
        "xin": np.ascontiguousarray(x_tok, dtype=np.float32),
        "cT": np.ascontiguousarray(np.stack([col(inp["c"][b])] * 2, -1)),
        "adaw": np.ascontiguousarray(inp["ada_w"][layer][:, 3072:6144]),
        "adab": col(inp["ada_b"][layer][3072:5120]),
        "adabg": np.ascontiguousarray(np.broadcast_to(inp["ada_b"][layer][5120:6144][None, :], (128, 1024))),
        "g2": col(inp["norm2_g"][layer]),
        "w1": np.ascontiguousarray(inp["mlp_w1"][layer]), "w2": np.ascontiguousarray(inp["mlp_w2"][layer]),
        "ident": np.eye(128, dtype=np.float32),
    }
    if final:
        d["fg"] = np.ascontiguousarray(np.broadcast_to(inp["final_g"][None, :], (128, 1024)))
    return d


EPS = 1e-6


def build_L4():
    nc = bass.Bass("TRN2", target_bir_lowering=False)
    D = lambda n, s, kind="ExternalInput": nc.dram_tensor(n, s, F32, kind=kind).ap()
    x1 = D("x1", [8192, 1024]); xown = D("xown", [4096, 1024])
    cT = D("cT", [128, 8, 2]); adaw = D("adaw", [1024, 3072]); adabg = D("adabg", [128, 3072]); g1 = D("g1", [128, 1024])
    fw = D("fw", [1024, 1024])
    cst = D("cst", [128, 256 + 256 + 256 + 128 + 128 + 128 + 512 + 512])
    xts = D("xts", [16, 128, 4096], "ExternalOutput")
    xout = D("xout", [4096, 1024], "ExternalOutput")
    k = KB(nc)
    with ExitStack() as es:
        T = lambda name, shape, dt: es.enter_context(nc.sbuf_tensor(name, shape, dt))
        P = lambda name, shape, dt: es.enter_context(nc.psum_tensor(name, shape, dt))
        cs = T("cs", [128, 8, 2], F32); srep = T("srep", [128, 8, 128], F32)
        mb = T("mb", [128, 3072], F32); g1s = T("g1s", [128, 1024], F32)
        stg = [T("stg%d" % i, [128, 1024], F32) for i in range(2)]
        cc_ = T("cc_", [128, 2176], F32)
        w1cs = T("w1cs", [128, 256], BF16); r3a = T("r3a", [128, 128], BF16); r3b = T("r3b", [128, 128], BF16)
        idb = T("idb", [128, 128], BF16); ccb = T("ccb", [128, 2, 256], BF16); scb = T("scb", [128, 2, 256], BF16)
        fwb = T("fwb", [128, 8, 1024], BF16)
        wpb = T("wpb", [128, 16, 1024], BF16)
        ssq = T("ssq", [128, 64], F32); rstd = T("rstd", [128, 64], F32)
        junk = T("junk", [128, 1024], BF16)
        xl = T("xl", [128, 16, 128], F32)
        hb = T("hb", [128, 128, 64], BF16)
        uu = T("uu", [128, 2, 256], F32); ww = T("ww", [128, 2, 256], F32)
        Br = [T("Br%d" % i, [128, 2, 128], BF16) for i in range(2)]
        Bi = [T("Bi%d" % i, [128, 2, 128], BF16) for i in range(2)]
        Xs = T("Xs", [128, 2, 32, 128], BF16)
        xtst = [T("xtst%d" % i, [128, 2, 128], F32) for i in range(2)]
        xtl = T("xtl", [128, 16, 128], F32); xtb = T("xtb", [128, 16, 128], BF16)
        xt = [T("xt%d" % i, [128, 1024], F32) for i in range(2)]
        tmp = T("tmp", [128, 512], F32)
        PA = P("PA", [128, 512], F32)
        PS1 = [P("PS1_%d" % i, [128, 512], F32) for i in range(2)]
        PS3 = [P("PS3_%d" % i, [128, 512], F32) for i in range(2)]
        pT = P("pT", [128, 4, 128], BF16)
        PO = [P("PO%d" % i, [128, 512], F32) for i in range(2)]

        k.dma("sp", cc_[:], cst, writes=["cc_"])
        k.cp("dve", w1cs[:], cc_[:, 0:256], ["cc_"], ["w1cs"])
        T1 = cc_[:, 256:512]; T2 = cc_[:, 512:768]
        k.cp("dve", r3a[:], cc_[:, 768:896], ["cc_"], ["r3a"]); k.cp("dve", r3b[:], cc_[:, 896:1024], ["cc_"], ["r3b"])
        k.cp("dve", idb[:], cc_[:, 1024:1152], ["cc_"], ["idb"])
        k.cp("dve", ccb[:].rearrange("p a b -> p (a b)"), cc_[:, 1152:1664], ["cc_"], ["ccb"])
        k.cp("dve", scb[:].rearrange("p a b -> p (a b)"), cc_[:, 1664:2176], ["cc_"], ["scb"])
        k.dma("sp", cs[:], cT, writes=["cs"])
        k.actf(srep[:], cs[:, :, 0:1].to_broadcast([128, 8, 128]), AF.Silu, ["cs"], ["srep"])
        k.dma("sp", mb[:], adabg, writes=["mb0"])
        k.dma("sp", g1s[:], g1, writes=["g1s"])
        adv = adaw.rearrange("(kc p) n -> p kc n", p=128)
        ri = 0
        for hf in range(6):
            for q in range(4):
                a = stg[ri % 2]; an = "stg%d" % (ri % 2); ri += 1
                av = a[:, 0:1024].rearrange("p (kc n) -> p kc n", kc=2)
                k.dma("sp" if q % 2 == 0 else "pool", av, adv[:, 2 * q:2 * q + 2, hf * 512:(hf + 1) * 512], writes=[an])
                for kk_ in range(2):
                    kc = 2 * q + kk_
                    k.mm(PA[:, 0:512], srep[:, kc, :], av[:, kk_, :], kc == 0, kc == 7, [an, "srep"], ["PA"])
            k.tt("dve", mb[:, hf * 512:(hf + 1) * 512], PA[:, 0:512], mb[:, hf * 512:(hf + 1) * 512], ALU.add, ["PA", "mb0"], ["mb"])
        k.stt("dve", mb[:, 1024:2048], mb[:, 1024:2048], 1.0, g1s[:], ALU.add, ALU.mult, ["mb", "g1s"], ["mb"])
        for kc in range(8):
            w = stg[ri % 2]; wn = "stg%d" % (ri % 2); ri += 1
            k.dma("sp" if kc % 2 == 0 else "pool", w[:], fw[kc * 128:(kc + 1) * 128, :], writes=[wn])
            k.cp("dve" if kc % 2 == 0 else "act", fwb[:, kc, :], w[:], [wn], ["fwb"])
        pi_ = 0
        for r_, tb in enumerate((ccb, scb)):
            tbn = "ccb" if r_ == 0 else "scb"
            for gch in range(8):
                grp = gch // 2; cl = gch % 2
                for hf in range(2):
                    po = PO[pi_ % 2]; pon = "PO%d" % (pi_ % 2); pi_ += 1
                    for kk_ in range(2):
                        k.mm(po[:], tb[:, kk_, cl * 128:(cl + 1) * 128], fwb[:, 2 * grp + kk_, hf * 512:(hf + 1) * 512], kk_ == 0, kk_ == 1,
                             [tbn, "fwb"], [pon])
                    k.cp("dve", wpb[:, r_ * 8 + gch, hf * 512:(hf + 1) * 512], po[:], [pon], ["wpb"])
        x1v = x1.rearrange("(l1 l2) c -> l1 l2 c", l2=64)
        for l2 in range(64):
            X = xt[l2 % 2]; Xn = "xt%d" % (l2 % 2)
            k.dma("sp" if l2 % 2 == 0 else "pool", X[:], x1v[:, l2, :], writes=[Xn])
            k.actf(junk[:], X[:], AF.Square, [Xn], ["junk", "ssq"], accum_out=ssq[:, l2:l2 + 1])
        k.ts("dve", rstd[:], ssq[:], 1.0 / 1024, EPS, ALU.mult, ALU.add, ["ssq"], ["rstd"])
        k.actf(rstd[:], rstd[:], AF.Sqrt, ["rstd"], ["rstd"])
        k.recip(rstd[:], rstd[:], ["rstd"], ["rstd"])
        out_toks = []
        tsc = [0]
        for cb in range(8):
            c0 = cb * 128
            for sb_ in range(4):
                l0 = sb_ * 16
                k.dma("sp", xl[:], x1v[:, l0:l0 + 16, c0:c0 + 128], writes=["xl"])
                k.tt("dve", xl[:], xl[:], rstd[:, l0:l0 + 16].unsqueeze(2).to_broadcast([128, 16, 128]), ALU.mult, ["xl", "rstd"], ["xl"])
                k.tt("pool", xl[:], xl[:], mb[:, 1024 + c0:1024 + c0 + 128].unsqueeze(1).to_broadcast([128, 16, 128]), ALU.mult,
                     ["xl", "mb"], ["xl"])
                k.tt("dve", hb[:, :, l0:l0 + 16].rearrange("p c l -> p l c"), xl[:], mb[:, c0:c0 + 128].unsqueeze(1).to_broadcast([128, 16, 128]), ALU.add,
                     ["xl", "mb"], ["hb"])
            for it in range(32):
                ps = PS1[it % 2]; psn = "PS1_%d" % (it % 2)
                for j2 in range(2):
                    j = 2 * it + j2
                    k.mm(ps[:, j2 * 256:(j2 + 1) * 256], hb[:, 2 * j:2 * j + 2, :].rearrange("p c l -> p (c l)"), w1cs[:], True, True,
                         ["hb", "w1cs"], [psn])
                psv = ps[:].rearrange("p (a b) -> p a b", a=2)
                k.tt("dve", uu[:], psv, T1.unsqueeze(1).to_broadcast([128, 2, 256]), ALU.mult, [psn, "cc_"], ["uu"])
                k.tt("dve", ww[:], psv, T2.unsqueeze(1).to_broadcast([128, 2, 256]), ALU.mult, [psn, "cc_"], ["ww"])
                br = Br[it % 2]; brn = "Br%d" % (it % 2); bi = Bi[it % 2]; bin_ = "Bi%d" % (it % 2)
                k.tt("pool", br[:], uu[:, :, 0:128], uu[:, :, 128:256], ALU.add, ["uu"], [brn])
                k.tt("pool", bi[:], ww[:, :, 0:128], ww[:, :, 128:256], ALU.add, ["ww"], [bin_])
                p3 = PS3[(it // 2) % 2]; p3n = "PS3_%d" % ((it // 2) % 2)
                for j2 in range(2):
                    q4 = (it % 2) * 2 + j2
                    k.mm(p3[:, q4 * 128:(q4 + 1) * 128], br[:, j2, :], r3a[:], True, False, [brn, "r3a"], [p3n])
                    k.mm(p3[:, q4 * 128:(q4 + 1) * 128], bi[:, j2, :], r3b[:], False, True, [bin_, "r3b"], [p3n])
                if it % 2 == 1:
                    for q4 in range(4):
                        ch = 4 * (it - 1) + 2 * q4
                        k.cp("dve", Xs[:, :, :, ch:ch + 2].rearrange("p r l c -> p r c l"),
                             p3[:, q4 * 128:(q4 + 1) * 128].rearrange("p (r c l) -> p r c l", r=2, c=2), [p3n], ["Xs"])
            for l2p in range(32):
                si = tsc[0] % 2; tsc[0] += 1
                S = xtst[si]; Sn = "xtst%d" % si
                for r_ in range(2):
                    k.tr(pT[:, r_, :], Xs[:, r_, l2p, :], idb[:], ["Xs", "idb"], ["pT"])
                k.cp("dve", S[:], pT[:, 0:2, :], ["pT"], [Sn])
                for r_ in range(2):
                    out_toks.append(k.dma("act" if r_ == 0 else "pool", xts[r_ * 8 + cb, :, l2p * 128:(l2p + 1) * 128],
                                          S[:, r_, :], reads=[Sn], writes=["xts_%d_%d_%d" % (l2p, cb, r_)]))
        for t in range(32):
            i = t % 2
            k.dma("sp", xtl[:], xts[:, :, t * 128:(t + 1) * 128].rearrange("a p t -> p a t"), reads=["xts_%d_%d_%d" % (t, cb_, r_) for cb_ in range(8) for r_ in range(2)], writes=["xtl"])
            k.cp("act", xtb[:], xtl[:], ["xtl"], ["xtb"])
            k.dma("pool", xt[i][:], xown[t * 128:(t + 1) * 128, :], writes=["xt%d" % i])
            for hf in range(2):
                po = PO[hf]; pon = "PO%d" % hf
                ns = slice(hf * 512, hf * 512 + 512)
                for kc in range(16):
                    k.mm(po[:], xtb[:, kc, :], wpb[:, kc, ns], kc == 0, kc == 15, ["xtb", "wpb"], [pon])
                k.tt("dve", tmp[:], po[:], mb[:, 2048 + hf * 512:2048 + (hf + 1) * 512], ALU.mult, [pon, "mb"], ["tmp"])
                k.tt("pool", xt[i][:, ns], tmp[:], xt[i][:, ns], ALU.add, ["tmp", "xt%d" % i], ["xt%d" % i])
            out_toks.append(k.dma("act", xout[t * 128:(t + 1) * 128, :], xt[i][:], reads=["xt%d" % i], writes=["xo%d" % t]))
        k.finish_wait("sp", out_toks)
        k.emit()
    print("L4 instrs", k.ninstr)
    return nc


def l4_consts(s):
    l1 = np.arange(128)[:, None]; lp1 = np.arange(128)[None, :]
    th = 2 * np.pi * (l1 * lp1 % 128) / 128
    W1 = np.concatenate([np.cos(th), np.sin(th)], 1)
    p = np.arange(128)[:, None]; l2 = p % 64
    ph = 2 * np.pi * (l2 * lp1) / 8192
    T1 = np.concatenate([np.cos(ph), -np.sin(ph)], 1); T2 = np.concatenate([np.sin(ph), np.cos(ph)], 1)
    lp2 = (32 * s + np.arange(32))[None, :]
    l2v = np.arange(64)[:, None]
    ps = 2 * np.pi * ((l2v * lp2) % 64) / 64
    nrm = 1.0 / np.sqrt(8192.0 * 256.0)
    C = np.cos(ps) * nrm; S = np.sin(ps) * nrm
    R3a = np.zeros((128, 2, 2, 32)); R3b = np.zeros((128, 2, 2, 32))
    for c2 in range(2):
        R3a[c2 * 64:(c2 + 1) * 64, 0, c2, :] = C; R3a[c2 * 64:(c2 + 1) * 64, 1, c2, :] = -S
        R3b[c2 * 64:(c2 + 1) * 64, 0, c2, :] = -S; R3b[c2 * 64:(c2 + 1) * 64, 1, c2, :] = -C
    c = np.arange(256)[:, None]; cp = np.arange(256)[None, :]
    a = 2 * np.pi * ((c * cp) % 256) / 256
    Cc = np.cos(a); Sc = np.sin(a)
    Ccl = Cc.reshape(2, 128, 256).transpose(1, 0, 2).reshape(128, 512)
    Scl = Sc.reshape(2, 128, 256).transpose(1, 0, 2).reshape(128, 512)
    return np.concatenate([W1, T1, T2, R3a.reshape(128, 128), R3b.reshape(128, 128), np.eye(128), Ccl, Scl], 1).astype(np.float32)


def l4_inputs(inp, b, s, x1_full):
    col = lambda v: np.ascontiguousarray(v.reshape(-1, 128).T)
    return {
        "x1": np.ascontiguousarray(x1_full), "xown": np.ascontiguousarray(x1_full[s * 4096:(s + 1) * 4096]),
        "cT": np.ascontiguousarray(np.stack([col(inp["c"][b])] * 2, -1)),
        "adaw": np.ascontiguousarray(inp["ada_w"][1][:, 0:3072]),
        "adabg": np.ascontiguousarray(np.broadcast_to(inp["ada_b"][1][0:3072][None, :], (128, 3072))),
        "g1": np.ascontiguousarray(np.broadcast_to(inp["norm1_g"][1][None, :], (128, 1024))),
        "fw": np.ascontiguousarray(inp["fourier_w_out"][0]),
        "cst": l4_consts(s),
    }


def _run(nc, ins):
    res = run_bass_kernel_spmd(nc, ins, core_ids=list(range(8)))
    return res.results


def kernel(**inputs):
    inp = {k_: np.asarray(v, dtype=np.float32) for k_, v in inputs.items()}
    r1 = _run(build_L1(), [l1_inputs(inp, c // 2, c % 2) for c in range(8)])
    zr = [np.asarray(r1[c]["zr"]) for c in range(8)]
    att = [np.asarray(r1[c]["att"]) for c in range(8)]
    del r1
    ins2 = []
    for c in range(8):
        b, d = c // 2, c % 2
        zfull = np.concatenate([zr[2 * b + s].reshape(1792, KW)[:, 128:128 + 4096].T for s in range(2)], 0)
        zc = zr[2 * b].reshape(1792, KW)[:, XW:XW + 256].T
        ins2.append(l2_inputs(inp, zfull, zc, d))
    r2 = _run(build_L2(), ins2)
    ys = [np.asarray(r2[c]["y"]) for c in range(8)]
    del r2, ins2
    ins3 = []
    for c in range(8):
        b, s = c // 2, c % 2
        yf = ys[2 * b][s * 4096:(s + 1) * 4096]
        yb = ys[2 * b + 1][::-1][s * 4096:(s + 1) * 4096]
        ins3.append(l3_inputs(inp, b, s, inp["x"][b, s * 4096:(s + 1) * 4096], att[c], yf, yb, zr[c]))
    r3 = _run(build_L3(), ins3)
    x1a = [np.asarray(r3[c]["xout"]) for c in range(8)]
    del r3, ins3, zr, att, ys
    r4 = _run(build_Lmlp(False), [lm_inputs(inp, 0, c // 2, x1a[c], False) for c in range(8)])
    x1 = [np.asarray(r4[c]["xout"]) for c in range(8)]
    del r4, x1a
    ins5 = []
    for c in range(8):
        b, s = c // 2, c % 2
        ins5.append(l4_inputs(inp, b, s, np.concatenate([x1[2 * b], x1[2 * b + 1]], 0)))
    r5 = _run(build_L4(), ins5)
    x2 = [np.asarray(r5[c]["xout"]) for c in range(8)]
    del r5, ins5, x1
    r6 = _run(build_Lmlp(True), [lm_inputs(inp, 1, c // 2, x2[c], True) for c in range(8)])
    out = np.zeros((4, 8192, 1024), np.float32)
    for c in range(8):
        b, s = c // 2, c % 2
        out[b, s * 4096:(s + 1) * 4096] = np.asarray(r6[c]["xout"])
    return out
```
